# Optimizing a Trainium2 kernel written in Bass

```python
import math
import jax, jax.numpy as jnp
from jax import lax
import numpy as np

D_MODEL = 1024
BATCH = 4
SEQ = 4096
DEPTH = 2

CHUNK = 64
Q_BLOCK = 128
PLE_DIM = 256
NORM_EPS = 1e-6
CONV_WIDTH = 256
CONV_K = 3
RWKV_HEADS = 4
RWKV_HEAD_DIM = 64
RWKV_WIDTH = RWKV_HEADS * RWKV_HEAD_DIM
DECAY_LORA = 64
ICLR_LORA = 64
DECAY_SCALE = math.exp(-0.5)
GN_EPS = 64e-5
RWKV_SHIFT_WIDTH = 3 * RWKV_WIDTH + DECAY_LORA + ICLR_LORA
MLA_HEADS = 4
QK_NOPE_DIM = 128
QK_ROPE_DIM = 64
V_HEAD_DIM = 128
Q_LORA_RANK = 384
KV_LORA_RANK = 256
MLA_WIDTH = MLA_HEADS * V_HEAD_DIM
ROPE_THETA = 10000.0
D_MIX = CONV_WIDTH + RWKV_WIDTH + MLA_WIDTH
IN_SPLITS = (CONV_WIDTH, CONV_WIDTH, CONV_WIDTH, CONV_WIDTH,
             RWKV_SHIFT_WIDTH, RWKV_WIDTH,
             Q_LORA_RANK, KV_LORA_RANK, QK_ROPE_DIM, MLA_WIDTH)
D_IN = sum(IN_SPLITS)

kernel_name = "hybrid_conv_rwkv7_mla_parallel_heads"


def _split(z, sizes):
    idx = [int(i) for i in np.cumsum(sizes)[:-1]]
    return jnp.split(z, idx, axis=-1)


def rmsnorm(x, g):
    xf = x.astype(jnp.float32)
    y = xf * lax.rsqrt(jnp.mean(xf * xf, axis=-1, keepdims=True) + NORM_EPS)
    return (y * g.astype(jnp.float32)).astype(x.dtype)


def rope(x, cos, sin):
    x1, x2 = jnp.split(x, 2, axis=-1)
    return jnp.concatenate([x1 * cos - x2 * sin, x2 * cos + x1 * sin], axis=-1)


def conv_branch(c_b, c_c, c_h, c_g, conv_w):
    u = c_c * c_h
    u = lax.conv_general_dilated(
        u, conv_w[:, None, :].astype(u.dtype), window_strides=(1,),
        padding=[(CONV_K - 1, 0)], dimension_numbers=("NWC", "WIO", "NWC"),
        feature_group_count=CONV_WIDTH)
    return c_b * u * jax.nn.silu(c_g)


def _rwkv_step(state, inp):
    r, w, k, v, za, zb = inp
    sa = jnp.einsum("bhvk,bhk->bhv", state, za)
    state = (state * w[:, :, None, :] + sa[..., None] * zb[:, :, None, :]
             + v[..., None] * k[:, :, None, :])
    y = jnp.einsum("bhvk,bhk->bhv", state, r)
    return state, y


def rwkv_branch(zs, g, mu, w0, w2, a0, a2, k_k, k_a, r_k, gn_w, gn_b):
    B, S, _ = zs.shape
    prev = jnp.pad(zs, ((0, 0), (1, 0), (0, 0)))[:, :-1]
    zs = zs + (prev - zs) * mu
    r, k, v, wd, ad = _split(zs, (RWKV_WIDTH, RWKV_WIDTH, RWKV_WIDTH, DECAY_LORA, ICLR_LORA))
    w = jnp.exp(-DECAY_SCALE * jax.nn.sigmoid(w0 + jnp.tanh(wd) @ w2))
    a = jax.nn.sigmoid(a0 + ad @ a2)
    hs = (B, S, RWKV_HEADS, RWKV_HEAD_DIM)
    r, k, v, w, a = (t.reshape(hs) for t in (r, k, v, w, a))
    kk = (k * k_k.reshape(RWKV_HEADS, RWKV_HEAD_DIM)).astype(jnp.float32)
    kk = kk * lax.rsqrt(jnp.sum(kk * kk, axis=-1, keepdims=True) + 1e-12)
    kk = kk.astype(k.dtype)
    k = k * (1.0 + (a - 1.0) * k_a.reshape(RWKV_HEADS, RWKV_HEAD_DIM))
    za = -kk
    zb = kk * a
    xs = tuple(jnp.swapaxes(t, 0, 1).astype(jnp.float32) for t in (r, w, k, v, za, zb))
    s0 = jnp.zeros((B, RWKV_HEADS, RWKV_HEAD_DIM, RWKV_HEAD_DIM), jnp.float32)
    _, y = lax.scan(_rwkv_step, s0, xs)
    y = jnp.swapaxes(y, 0, 1)
    mean = jnp.mean(y, axis=-1, keepdims=True)
    var = jnp.mean(jnp.square(y - mean), axis=-1, keepdims=True)
    y = (y - mean) * lax.rsqrt(var + GN_EPS)
    y = (y * gn_w.reshape(RWKV_HEADS, RWKV_HEAD_DIM).astype(jnp.float32)
         + gn_b.reshape(RWKV_HEADS, RWKV_HEAD_DIM).astype(jnp.float32)).astype(v.dtype)
    bonus = jnp.sum(r * k * r_k, axis=-1, keepdims=True) * v
    y = (y + bonus).reshape(B, S, RWKV_WIDTH)
    return y * jax.nn.silu(g)


def chunk_causal_attention(q, k, v):
    B, S, H, Dk = q.shape
    Dv = v.shape[-1]
    nq = S // Q_BLOCK
    qb = jnp.moveaxis(q.reshape(B, nq, Q_BLOCK, H, Dk), 1, 0)
    key_chunk = jnp.arange(S) // CHUNK
    q_chunk = key_chunk.reshape(nq, Q_BLOCK)
    scale = 1.0 / math.sqrt(Dk)

    def one_block(args):
        qi, qc = args
        s = jnp.einsum("bqhd,bkhd->bhqk", qi, k).astype(jnp.float32) * scale
        mask = key_chunk[None, :] <= qc[:, None]
        s = jnp.where(mask, s, jnp.finfo(jnp.float32).min)
        pr = jax.nn.softmax(s, axis=-1).astype(v.dtype)
        return jnp.einsum("bhqk,bkhd->bqhd", pr, v)

    o = lax.map(one_block, (qb, q_chunk))
    return jnp.moveaxis(o, 0, 1).reshape(B, S, H * Dv)


def mla_branch(qa, kva, krope, g, q_norm_g, w_qb, kv_norm_g, w_kvb, cos, sin):
    B, S, _ = qa.shape
    q = (rmsnorm(qa, q_norm_g) @ w_qb).reshape(B, S, MLA_HEADS, QK_NOPE_DIM + QK_ROPE_DIM)
    q_nope, q_rope = q[..., :QK_NOPE_DIM], q[..., QK_NOPE_DIM:]
    q_rope = rope(q_rope, cos[:, :, None, :], sin[:, :, None, :])
    kv = (rmsnorm(kva, kv_norm_g) @ w_kvb).reshape(B, S, MLA_HEADS, QK_NOPE_DIM + V_HEAD_DIM)
    k_nope, v = kv[..., :QK_NOPE_DIM], kv[..., QK_NOPE_DIM:]
    k_rope = rope(krope, cos, sin)
    q = jnp.concatenate([q_nope, q_rope], axis=-1)
    k = jnp.concatenate(
        [k_nope, jnp.broadcast_to(k_rope[:, :, None, :], (B, S, MLA_HEADS, QK_ROPE_DIM))], axis=-1)
    o = chunk_causal_attention(q, k, v)
    return o * jax.nn.silu(g)


def setup_inputs(seed: int = 0) -> dict:
    key = jax.random.key(seed)
    ks = jax.random.split(key, 32)
    f32 = jnp.float32
    nrm = lambda k, shape, s: jax.random.normal(k, shape, f32) * s
    L = DEPTH
    offsets = jax.random.randint(ks[2], (BATCH, 1), 0, 64) * CHUNK
    positions = (offsets + jnp.arange(SEQ)[None, :]).astype(jnp.int32)
    return {
        "x": nrm(ks[0], (BATCH, SEQ, D_MODEL), 1.0),
        "p": nrm(ks[1], (DEPTH, BATCH, SEQ, PLE_DIM), 1.0),
        "positions": positions,
        "norm_mix_g": 1.0 + nrm(ks[3], (L, D_MODEL), 0.02),
        "w_in": nrm(ks[4], (L, D_MODEL, D_IN), D_MODEL ** -0.5),
        "conv_w": nrm(ks[5], (L, CONV_K, CONV_WIDTH), CONV_K ** -0.5),
        "rwkv_mu": jax.random.uniform(ks[6], (L, RWKV_SHIFT_WIDTH), f32),
        "rwkv_w0": nrm(ks[7], (L, RWKV_WIDTH), 1.0),
        "rwkv_w2": nrm(ks[8], (L, DECAY_LORA, RWKV_WIDTH), 0.1 * DECAY_LORA ** -0.5),
        "rwkv_a0": nrm(ks[9], (L, RWKV_WIDTH), 0.1),
        "rwkv_a2": nrm(ks[10], (L, ICLR_LORA, RWKV_WIDTH), 0.1 * ICLR_LORA ** -0.5),
        "rwkv_kk": 0.85 + nrm(ks[11], (L, RWKV_WIDTH), 0.05),
        "rwkv_ka": 1.0 + nrm(ks[12], (L, RWKV_WIDTH), 0.05),
        "rwkv_rk": nrm(ks[13], (L, RWKV_HEADS, RWKV_HEAD_DIM), 0.1),
        "rwkv_gn_w": 1.0 + nrm(ks[14], (L, RWKV_WIDTH), 0.02),
        "rwkv_gn_b": nrm(ks[15], (L, RWKV_WIDTH), 0.02),
        "mla_q_norm_g": 1.0 + nrm(ks[16], (L, Q_LORA_RANK), 0.02),
        "mla_w_qb": nrm(ks[17], (L, Q_LORA_RANK, MLA_HEADS * (QK_NOPE_DIM + QK_ROPE_DIM)), Q_LORA_RANK ** -0.5),
        "mla_kv_norm_g": 1.0 + nrm(ks[18], (L, KV_LORA_RANK), 0.02),
        "mla_w_kvb": nrm(ks[19], (L, KV_LORA_RANK, MLA_HEADS * (QK_NOPE_DIM + V_HEAD_DIM)), KV_LORA_RANK ** -0.5),
        "w_out": nrm(ks[20], (L, D_MIX, D_MODEL), 0.5 * D_MIX ** -0.5),
        "ple_w": nrm(ks[21], (L, PLE_DIM, D_MODEL), 0.5 * PLE_DIM ** -0.5),
        "ple_norm_g": 1.0 + nrm(ks[22], (L, D_MODEL), 0.02),
        "ple_gate_w": nrm(ks[23], (L, D_MODEL, D_MODEL), D_MODEL ** -0.5),
        "final_norm_g": 1.0 + nrm(ks[24], (D_MODEL,), 0.02),
    }


def reference(x, p, positions, norm_mix_g, w_in, conv_w, rwkv_mu, rwkv_w0, rwkv_w2,
              rwkv_a0, rwkv_a2, rwkv_kk, rwkv_ka, rwkv_rk, rwkv_gn_w, rwkv_gn_b,
              mla_q_norm_g, mla_w_qb, mla_kv_norm_g, mla_w_kvb, w_out,
              ple_w, ple_norm_g, ple_gate_w, final_norm_g):
    inv_freq = 1.0 / (ROPE_THETA ** (jnp.arange(0, QK_ROPE_DIM, 2, dtype=jnp.float32) / QK_ROPE_DIM))
    ang = positions.astype(jnp.float32)[..., None] * inv_freq
    cos = jnp.cos(ang).astype(x.dtype)
    sin = jnp.sin(ang).astype(x.dtype)

    h = x
    for i in range(DEPTH):
        u = rmsnorm(h, norm_mix_g[i])
        z = u @ w_in[i]
        (c_b, c_c, c_h, c_g, r_shift, r_g, m_qa, m_kva, m_krope, m_g) = _split(z, IN_SPLITS)
        y_conv = conv_branch(c_b, c_c, c_h, c_g, conv_w[i])
        y_rwkv = rwkv_branch(r_shift, r_g, rwkv_mu[i], rwkv_w0[i], rwkv_w2[i], rwkv_a0[i],
                             rwkv_a2[i], rwkv_kk[i], rwkv_ka[i], rwkv_rk[i],
                             rwkv_gn_w[i], rwkv_gn_b[i])
        y_mla = mla_branch(m_qa, m_kva, m_krope, m_g, mla_q_norm_g[i], mla_w_qb[i],
                           mla_kv_norm_g[i], mla_w_kvb[i], cos, sin)
        y = jnp.concatenate([y_conv, y_rwkv, y_mla], axis=-1) @ w_out[i]
        h = h + y
        gate = jax.nn.sigmoid(rmsnorm(h, ple_norm_g[i]) @ ple_gate_w[i])
        h = h + (p[i] @ ple_w[i]) * gate
    return rmsnorm(h, final_norm_g)
```

```python
import math
import contextlib
import numpy as np
import concourse.bass as bass
import concourse.mybir as mybir
from concourse.bass_utils import run_bass_kernel_spmd

F32 = mybir.dt.float32
BF16 = mybir.dt.bfloat16
I32 = mybir.dt.int32
AF = mybir.ActivationFunctionType
ALU = mybir.AluOpType

SAME_ENGINE_SYNC = True
SERIALIZE_ROW_SWITCH = True

D_MODEL = 1024
DEPTH = 2
BATCH = 4
SEQ = 4096
TT = 512
NORM_EPS = 1e-6
GN_EPS = 64e-5
DECAY_SCALE = math.exp(-0.5)
ROPE_THETA = 10000.0
NWCH = 28


class Tn:
    def __init__(self, h, name, dsem=None):
        self.h = h; self.name = name; self.lastw = None; self.readers = {}; self.dsem = dsem

    def __getitem__(self, idx):
        return V(self.h[idx], self)

    @property
    def all(self):
        return V(self.h.ap(), self)


class V:
    def __init__(self, ap, buf):
        self.ap = ap; self.buf = buf

    def bc(self, shape):
        return V(self.ap.broadcast_to(list(shape)), self.buf)

    def bitcast(self, dt):
        return V(self.ap.bitcast(dt), self.buf)

    def __getitem__(self, idx):
        return V(self.ap[idx], self.buf)

    def rr(self, s, **kw):
        return V(self.ap.rearrange(s, **kw), self.buf)


class Sched:
    def __init__(self, nc, stack):
        self.nc = nc; self.stack = stack
        self.streams = {e: [] for e in ('pe', 'act', 'dve', 'pool', 'sp')}
        self.count = {e: 0 for e in ('pe', 'act', 'dve', 'pool')}
        self.dcount = {}
        self.known = {e: {} for e in self.streams}
        self.ndsem = 0
        self.sbuf_bytes = 0

    def new_dsem(self):
        self.ndsem += 1
        k = 'dma%d' % self.ndsem
        self.dcount[k] = 0
        return k

    def sbuf(self, name, shape, dt, dsem=False):
        h = self.stack.enter_context(self.nc.sbuf_tensor("s_" + name, list(shape), dt))
        n = 1
        for s in shape[1:]:
            n *= s
        self.sbuf_bytes += n * (4 if dt in (F32, I32) else 2)
        return Tn(h, name, self.new_dsem() if dsem else None)

    def psum(self, name, shape, dt):
        h = self.stack.enter_context(self.nc.psum_tensor(name, list(shape), dt))
        return Tn(h, name)

    def dram(self, name, shape, dt, kind):
        h = self.nc.dram_tensor(name, list(shape), dt, kind=kind)
        return Tn(h, name)

    def _need(self, eng, tok, waits):
        if tok is None:
            return
        key, val = tok
        if key == eng and (eng == 'pe' or not SAME_ENGINE_SYNC):
            return
        if key in self.dcount:
            val = max(val, self.dcount[key])
        if self.known[eng].get(key, 0) >= val:
            return
        self.known[eng][key] = val
        waits.append((key, val))

    def op(self, eng, fn, reads=(), writes=(), dma=None, extra_waits=()):
        waits = [tuple(w) for w in extra_waits]
        rb = []
        for v in reads:
            b = v.buf if isinstance(v, V) else v
            if b not in rb:
                rb.append(b)
        wb = []
        for v in writes:
            b = v.buf if isinstance(v, V) else v
            if b not in wb:
                wb.append(b)
        for b in rb:
            self._need(eng, b.lastw, waits)
        for b in wb:
            self._need(eng, b.lastw, waits)
            for k, val in b.readers.items():
                self._need(eng, (k, val), waits)
        if dma is None:
            self.count[eng] += 1
            tok = (eng, self.count[eng])
        else:
            self.dcount[dma] += 16
            tok = (dma, self.dcount[dma])
        for b in rb:
            if b in wb:
                continue
            if b.readers.get(tok[0], 0) < tok[1]:
                b.readers[tok[0]] = tok[1]
        for b in wb:
            b.lastw = tok
            b.readers = {}
        self.streams[eng].append((waits, fn, tok, dma is not None))
        return tok

    def dma(self, eng, out, in_, **kw):
        sem = out.buf.dsem if out.buf.dsem is not None else in_.buf.dsem
        assert sem is not None, (out.buf.name, in_.buf.name)
        o, i = out.ap, in_.ap
        return self.op(eng, lambda e: e.dma_start(out=o, in_=i, **kw), reads=[in_], writes=[out], dma=sem)

    def mm(self, out, lhsT, rhs, start=True, stop=True, **kw):
        o, l, r = out.ap, lhsT.ap, rhs.ap
        r0 = int(l.start_partition()); r1 = r0 + int(l.partition_size())
        prev = getattr(self, '_prev_mm', None)
        extra = []
        if prev is not None:
            (p0, p1, pbank, ptok) = prev
            if (r1 <= p0 or p1 <= r0) and (pbank is out.buf or SERIALIZE_ROW_SWITCH):
                extra.append(ptok)
                self.n_serial = getattr(self, 'n_serial', 0) + 1
        tok = self.op('pe', lambda e: e.matmul(o, l, r, start=start, stop=stop, **kw),
                      reads=[lhsT, rhs], writes=[out], extra_waits=extra)
        self._prev_mm = (r0, r1, out.buf, tok)
        return tok

    def transpose(self, out, in_, ident):
        o, i, d = out.ap, in_.ap, ident.ap
        self._prev_mm = None
        return self.op('pe', lambda e: e.transpose(o, i, d), reads=[in_, ident], writes=[out])

    def act(self, out, in_, func, bias=None, scale=None):
        o, i = out.ap, in_.ap
        kw = {}
        rd = [in_]
        if bias is not None:
            if isinstance(bias, V):
                kw['bias'] = bias.ap; rd.append(bias)
            else:
                kw['bias'] = bias
        if scale is not None:
            if isinstance(scale, V):
                kw['scale'] = scale.ap; rd.append(scale)
            else:
                kw['scale'] = scale
        return self.op('act', lambda e: e.activation(o, i, func, **kw), reads=rd, writes=[out])

    def tt(self, eng, out, in0, in1, op):
        o, a, b = out.ap, in0.ap, in1.ap
        return self.op(eng, lambda e: e.tensor_tensor(o, a, b, op), reads=[in0, in1], writes=[out])

    def ts(self, eng, out, in0, s1, op0, s2=None, op1=None):
        o, a = out.ap, in0.ap
        rd = [in0]
        if isinstance(s1, V):
            rd.append(s1); s1 = s1.ap
        if isinstance(s2, V):
            rd.append(s2); s2 = s2.ap
        if op1 is None:
            return self.op(eng, lambda e: e.tensor_scalar(o, a, s1, None, op0), reads=rd, writes=[out])
        return self.op(eng, lambda e: e.tensor_scalar(o, a, s1, s2, op0, op1), reads=rd, writes=[out])

    def stt(self, out, in0, scalar, in1, op0, op1):
        o, a, b = out.ap, in0.ap, in1.ap
        rd = [in0, in1]
        if isinstance(scalar, V):
            rd.append(scalar); scalar = scalar.ap
        return self.op('dve', lambda e: e.scalar_tensor_tensor(o, a, scalar, b, op0, op1), reads=rd, writes=[out])

    def copy(self, eng, out, in_):
        o, i = out.ap, in_.ap
        if eng == 'act':
            return self.op('act', lambda e: e.copy(o, i), reads=[in_], writes=[out])
        return self.op(eng, lambda e: e.tensor_copy(o, i), reads=[in_], writes=[out])

    def memset(self, eng, out, val):
        o = out.ap
        return self.op(eng, lambda e: e.memset(o, val), reads=[], writes=[out])

    def scan(self, out, d0, d1, init, op0, op1):
        o, a, b = out.ap, d0.ap, d1.ap
        return self.op('dve', lambda e: e.tensor_tensor_scan(o, a, b, init, op0, op1), reads=[d0, d1], writes=[out])

    def recip(self, out, in_):
        o, i = out.ap, in_.ap
        return self.op('dve', lambda e: e.reciprocal(o, i), reads=[in_], writes=[out])

    def emit(self, final_wait_bufs=()):
        nc = self.nc
        sems = {}
        for k in list(self.count.keys()) + list(self.dcount.keys()):
            sems[k] = self.stack.enter_context(nc.semaphore(k))
        finals = []
        for b in final_wait_bufs:
            if b.lastw is not None:
                finals.append(b.lastw)
        streams = self.streams

        def run(name, e):
            for waits, fn, tok, isdma in streams[name]:
                for key, val in waits:
                    e.wait_ge(sems[key], val)
                ins = fn(e)
                ins.then_inc(sems[tok[0]], 16 if isdma else 1)
            if name == 'sp':
                for key, val in finals:
                    e.wait_ge(sems[key], max(val, self.dcount.get(key, 0)))

        with nc.Block() as block:
            @block.tensor
            def _(e):
                run('pe', e)

            @block.scalar
            def _(e):
                run('act', e)

            @block.vector
            def _(e):
                run('dve', e)

            @block.gpsimd
            def _(e):
                run('pool', e)

            @block.sync
            def _(e):
                run('sp', e)
        for k, v in self.count.items():
            assert v < 60000, (k, v)
        for k, v in self.dcount.items():
            assert v < 60000, (k, v)


PP_G1 = 0
PP_G2 = 8
PP_CW = 16
PP_MU = 22
PP_W0 = 29
PP_A0 = 31
PP_KK = 33
PP_KA = 35
PP_RK = 37
PP_GNW = 39
PP_GNB = 41
PP_GQ = 43
PP_GKV = 46
NPP = 48

CM_ONES = 0
CM_BONES = 128
CM_IDENT = 256
CM_MASK2 = 384
CM_MASKTS = 640
CM_DMASK = 768
CM_SCAN = 896
CM_INVF = 1408
CM_SSIGN = 1409
NCM = 1410


class _StopBuild(Exception):
    pass


def build_program(S, L=DEPTH, dbg=None, wseq=None):
    NT = S // TT
    NB = S // 128
    nc = bass.Bass("TRN2", target_bir_lowering=False)
    st = contextlib.ExitStack()
    with st:
        sc = Sched(nc, st)
        xT = sc.dram("xT", [D_MODEL, S], F32, "ExternalInput")
        pT = sc.dram("pT", [L, 256, S], F32, "ExternalInput")
        posd = sc.dram("pos", [128, S], I32, "ExternalInput")
        ppd = sc.dram("pp", [L, 128, NPP], F32, "ExternalInput")
        fgd = sc.dram("fg", [128, 8], F32, "ExternalInput")
        cmd = sc.dram("cmat", [128, NCM], F32, "ExternalInput")
        wind = sc.dram("win", [L, NWCH, 128, 1024], F32, "ExternalInput")
        w2a2d = sc.dram("w2a2", [L, 128, 256], F32, "ExternalInput")
        wqbd = sc.dram("wqb", [L, 8, 128, 384], F32, "ExternalInput")
        wkvkd = sc.dram("wkvk", [L, 4, 128, 256], F32, "ExternalInput")
        wkvvd = sc.dram("wkvv", [L, 128, 1024], F32, "ExternalInput")
        woutd = sc.dram("wout", [L, 8, 128, 1024], F32, "ExternalInput")
        wgated = sc.dram("wgate", [L, 8, 128, 1024], F32, "ExternalInput")
        wpled = sc.dram("wple", [L, 8, 128, 256], F32, "ExternalInput")
        outT = sc.dram("outT", [D_MODEL, S], F32, "ExternalOutput")
        hscr = sc.dram("hscr", [D_MODEL, S], F32, "Internal")

        cm = sc.sbuf("cm", [128, NCM - 896], F32, dsem=True)
        cb = sc.sbuf("cb", [128, 896], BF16, dsem=True)
        pp = [sc.sbuf("pp%d" % l, [128, NPP], F32, dsem=True) for l in range(L)]
        fg = sc.sbuf("fg", [128, 8], F32, dsem=True)
        KnT = [sc.sbuf("KnT%d" % h, [128, S], BF16) for h in range(4)]
        KrT = sc.sbuf("KrT", [128, S], BF16)
        Vtok = sc.sbuf("Vtok", [128, NB, 512], BF16)
        NWB = 5
        wbuf = [sc.sbuf("wbuf%d" % i, [128, 1024], BF16, dsem=True) for i in range(NWB)]
        G = [sc.sbuf("G%d" % i, [128, 514], F32, dsem=True) for i in range(8)]
        X = [sc.sbuf("X%d" % i, [128, 512], F32, dsem=True) for i in range(7)]
        uT = [sc.sbuf("uT%d" % i, [128, 512], BF16) for i in range(8)]
        ymix = [sc.sbuf("ymix%d" % i, [128, 512], BF16) for i in range(8)]
        B16 = [sc.sbuf("B16_%d" % i, [128, 512], BF16) for i in range(12)]
        PB = [sc.sbuf("PB%d" % i, [128, 512], BF16, dsem=True) for i in range(2)]
        ARs = [sc.sbuf("AR%d" % f, [128, 2, 512], BF16) for f in range(2)]
        RB = [[sc.sbuf("RB%d_%d" % (f, i), [128, 512], BF16) for i in range(5)] for f in range(2)]
        EG = [sc.sbuf("EG%d" % f, [128, 512], F32, dsem=True) for f in range(2)]
        BV = [sc.sbuf("BV%d" % f, [128, 512], F32) for f in range(2)]
        rstd = sc.sbuf("rstd", [128, 512], F32)
        XT = [sc.sbuf("XT%d" % i, [128, 512], F32) for i in range(2)]
        cosT = sc.sbuf("cosT", [128, 512], F32)
        sinT = sc.sbuf("sinT", [128, 512], F32)
        ucv = [sc.sbuf("ucv%d" % c, [128, 514], F32) for c in range(2)]
        zraw = [G[1], G[2]]
        zhalo = sc.sbuf("zhalo", [128, 8], F32)
        w2a2 = sc.sbuf("w2a2", [128, 256], BF16, dsem=True)
        tokms = [[sc.sbuf("tokm%d_%d" % (f, b), [128, 4, 128], BF16) for b in range(4)] for f in range(2)]
        CM = []
        for f in range(2):
            CM.append(dict(
                NBm=sc.sbuf("NBm%d" % f, [128, 2, 256], BF16), KBm=sc.sbuf("KBm%d" % f, [128, 2, 256], BF16),
                Lm=[sc.sbuf("Lm%d_%d" % (f, i), [128, 2, 128], BF16) for i in range(2)],
                Nm=[sc.sbuf("Nm%d_%d" % (f, i), [128, 2, 128], BF16) for i in range(2)],
                Pm=[sc.sbuf("Pm%d_%d" % (f, i), [128, 2, 128], BF16) for i in range(2)],
                W1b=sc.sbuf("W1b%d" % f, [128, 2, 64], BF16), U0f=sc.sbuf("U0f%d" % f, [128, 2, 64], F32),
                Ub=sc.sbuf("Ub%d" % f, [128, 2, 64], BF16), AhT=sc.sbuf("AhT%d" % f, [128, 128], BF16)))
        Hf = [[sc.sbuf("Hf%d_%d" % (fc, i), [128, 64], F32) for i in range(2)] for fc in range(2)]
        Hb = [[sc.sbuf("Hb%d_%d" % (fc, i), [128, 128], BF16) for i in range(2)] for fc in range(2)]
        PS = [sc.psum("ps%d" % i, [128, 512], F32) for i in range(8)]
        dbgbuf = None

        ones_bf = cb[:, CM_ONES:CM_ONES + 128]
        bones_bf = cb[:, CM_BONES:CM_BONES + 128]
        ident_bf = cb[:, CM_IDENT:CM_IDENT + 128]
        mask2_bf = cb[:, CM_MASK2:CM_MASK2 + 256]
        maskTS_bf = cb[:, CM_MASKTS:CM_MASKTS + 128]
        dmask_bf = cb[:, CM_DMASK:CM_DMASK + 128]
        scanm = cm[:, CM_SCAN - 896:CM_SCAN - 896 + 512]
        invf = cm[:, CM_INVF - 896:CM_INVF - 896 + 1]
        ssign = cm[:, CM_SSIGN - 896:CM_SSIGN - 896 + 1]

        sc.dma('sp', cm.all, cmd[:, 896:NCM])
        sc.dma('pool', cb.all, cmd[:, 0:896])
        sc.dma('sp', fg.all, fgd.all)
        for l in range(L):
            sc.dma('sp', pp[l].all, ppd[l])

        wstate = {'i': 0, 'issued': 0}
        wrec = []
        build_program.wrec = wrec
        dramw = {'win': wind, 'wqb': wqbd, 'wkvk': wkvkd, 'wkvv': wkvvd, 'wout': woutd, 'wgate': wgated, 'wple': wpled}
        LOOKAHEAD = NWB - 1

        def _issue(k, key, ncols):
            name, idx = key
            sc.dma('pool', wbuf[k % NWB][:, 0:ncols], dramw[name][idx])

        def wload(key, ncols):
            i = wstate['i']
            wstate['i'] += 1
            wrec.append((key, ncols))
            if wseq is None:
                _issue(i, key, ncols)
            else:
                assert wseq[i] == (key, ncols), (i, wseq[i], key)
                hi = min(i + LOOKAHEAD, len(wseq) - 1)
                while wstate['issued'] <= hi:
                    k = wstate['issued']
                    _issue(k, wseq[k][0], wseq[k][1])
                    wstate['issued'] += 1
            return wbuf[i % NWB]

        def proj(key, K, rhs, ps):
            slot = wload(key, K * 128)
            for k in range(K):
                sc.mm(ps.all, slot[:, k * 128:(k + 1) * 128], rhs[k], start=(k == 0), stop=(k == K - 1))
            return ps

        psi = {'i': 0, 'n': 6}

        def nps():
            p = PS[psi['i'] % psi['n']]
            psi['i'] += 1
            return p

        def rsqrt_bc(out, ps, scale, eps):
            sc.act(out, ps, AF.Ln, bias=eps_ap[eps], scale=scale)
            sc.act(out, out, AF.Exp, scale=-0.5)

        cvals = sc.sbuf("cvals", [128, 8], F32)
        eps_ap = {}
        for i, v in enumerate([NORM_EPS, 1e-12, GN_EPS, math.pi / 2, 0.0]):
            sc.memset('dve', cvals[:, i:i + 1], float(v))
            eps_ap[v] = cvals[:, i:i + 1]

        marks = []
        build_program.marks = marks

        def stage(name):
            marks.append((name, sc.count['pe']))
            if dbg is not None and dbg == name:
                for i in range(8):
                    sc.dma('sp', outT[i * 128:(i + 1) * 128, 0:512], G[i][:, 0:512])
                raise _StopBuild()

        try:
          for l in range(L):
              src = xT if l == 0 else hscr
              P = pp[l]

              def pc(i):
                  return P[:, i:i + 1]

              sc.dma('pool', w2a2.all, w2a2d[l])
              for c in range(2):
                  sc.memset('dve', ucv[c][:, 0:2], 0.0)
              sc.memset('dve', zhalo.all, 0.0)
              hcur = [0, 0]
              for fc in range(2):
                  sc.memset('dve', Hf[fc][0].all, 0.0)
                  sc.memset('dve', Hb[fc][0].all, 0.0)
                  sc.memset('dve', Hb[fc][1].all, 0.0)

              for j in range(NT):
                  t0 = j * TT
                  tsl = slice(t0, t0 + TT)

                  HN = [X[0], X[1], X[2], X[3], X[4], X[5], EG[0], EG[1]]
                  if j > 0:
                      hsrc = [HN[i].all for i in range(8)]
                  else:
                      for i in range(8):
                          sc.dma('sp', G[i][:, 0:512], src[i * 128:(i + 1) * 128, tsl])
                      hsrc = [G[i][:, 0:512] for i in range(8)]
                  psA = nps()
                  for i in range(8):
                      sq = B16[i % 2]
                      sc.act(sq.all, hsrc[i], AF.Square)
                      sc.mm(psA.all, ones_bf, sq.all, start=(i == 0), stop=(i == 7))
                  rsqrt_bc(rstd.all, psA.all, 1.0 / D_MODEL, NORM_EPS)
                  for i in range(8):
                      sc.stt(uT[i].all, hsrc[i], pc(PP_G1 + i), rstd.all, ALU.mult, ALU.mult)
                  urhs = [uT[i].all for i in range(8)]

                  stage('A')
                  posiV = X[3].all.bitcast(I32)
                  sc.dma('sp', posiV, posd[:, tsl])
                  ang = X[6]
                  sc.copy('dve', ang.all, posiV)
                  sc.ts('dve', ang.all, ang.all, invf, ALU.mult)
                  for (dst, off) in ((sinT, 0.0), (cosT, 0.25)):
                      m = G[0]
                      sc.ts('dve', m[:, 0:512], ang.all, 1.0 / (2 * math.pi), ALU.mult, off, ALU.add)
                      sc.copy('dve', posiV, m[:, 0:512])
                      sc.copy('dve', m[:, 0:512], posiV)
                      C1 = 6.28125
                      C2 = 2 * math.pi - C1
                      sc.stt(G[1][:, 0:512], m[:, 0:512], -C1, ang.all, ALU.mult, ALU.add)
                      sc.stt(G[1][:, 0:512], m[:, 0:512], -C2, G[1][:, 0:512], ALU.mult, ALU.add)
                      PI_LO = 3.1415925
                      if off == 0.0:
                          sc.ts('dve', G[1][:, 0:512], G[1][:, 0:512], -PI_LO, ALU.max, PI_LO, ALU.min)
                          sc.act(dst.all, G[1][:, 0:512], AF.Sin)
                          sc.ts('dve', dst.all, dst.all, ssign, ALU.mult)
                      else:
                          sc.ts('dve', G[1][:, 0:512], G[1][:, 0:512], math.pi / 2, ALU.add, -PI_LO, ALU.max)
                          sc.ts('dve', G[1][:, 0:512], G[1][:, 0:512], PI_LO, ALU.min)
                          sc.act(dst.all, G[1][:, 0:512], AF.Sin)

                  def rope_combine(dst, ps1, ps2):
                      sc.tt('dve', G[2][:, 0:512], ps1.all, cosT.all, ALU.mult)
                      sc.tt('dve', G[3][:, 0:512], ps2.all, sinT.all, ALU.mult)
                      sc.tt('dve', dst, G[2][:, 0:512], G[3][:, 0:512], ALU.add)

                  stage('rot')
                  WQA, WKVA, WKR, WGM = 17, 20, 22, 24
                  sqq = [B16[0], B16[1], B16[11]]; sqk = [B16[9], B16[10]]
                  for c in range(3):
                      ps = proj(('win', (l, WQA + c)), 8, urhs, nps())
                      sc.copy('act', G[4 + c][:, 0:512], ps.all)
                      sc.act(sqq[c].all, G[4 + c][:, 0:512], AF.Square)
                  for c in range(2):
                      ps = proj(('win', (l, WKVA + c)), 8, urhs, nps())
                      sc.copy('act', G[c][:, 0:512], ps.all)
                      sc.act(sqk[c].all, G[c][:, 0:512], AF.Square)
                  psBq = nps()
                  for c in range(3):
                      sc.mm(psBq.all, ones_bf, sqq[c].all, start=(c == 0), stop=(c == 2))
                  psBk = nps()
                  for c in range(2):
                      sc.mm(psBk.all, ones_bf, sqk[c].all, start=(c == 0), stop=(c == 1))
                  rsqrt_bc(rstd.all, psBq.all, 1.0 / 384, NORM_EPS)
                  rsqrt_bc(G[7][:, 0:512], psBk.all, 1.0 / 256, NORM_EPS)
                  qn = [B16[2 + c] for c in range(3)]
                  for c in range(3):
                      sc.stt(qn[c].all, G[4 + c][:, 0:512], pc(PP_GQ + c), rstd.all, ALU.mult, ALU.mult)
                  kvn = [B16[5 + c] for c in range(2)]
                  for c in range(2):
                      sc.stt(kvn[c].all, G[c][:, 0:512], pc(PP_GKV + c), G[7][:, 0:512], ALU.mult, ALU.mult)
                  ps1 = proj(('win', (l, WKR)), 8, urhs, nps())
                  ps2 = proj(('win', (l, WKR + 1)), 8, urhs, nps())
                  rope_combine(KrT[:, tsl], ps1, ps2)
                  SG = [B16[7 + c] for c in range(4)]
                  for c in range(4):
                      ps = proj(('win', (l, WGM + c)), 8, urhs, nps())
                      sc.act(SG[c].all, ps.all, AF.Silu)
                  qrhs = [qn[c].all for c in range(3)]
                  Qn = [ymix[h] for h in range(4)]
                  for h in range(4):
                      ps = proj(('wqb', (l, h)), 3, qrhs, nps())
                      sc.copy('act', Qn[h].all, ps.all)
                  Qr = [B16[0], B16[1]]
                  for c in range(2):
                      ps1 = proj(('wqb', (l, 4 + c)), 3, qrhs, nps())
                      ps2 = proj(('wqb', (l, 6 + c)), 3, qrhs, nps())
                      rope_combine(Qr[c].all, ps1, ps2)
                  kvrhs = [kvn[c].all for c in range(2)]
                  for h in range(4):
                      ps = proj(('wkvk', (l, h)), 2, kvrhs, nps())
                      sc.copy('act', KnT[h][:, tsl], ps.all)
                  slot = wload(('wkvv', (l,)), 1024)
                  for blk in range(4):
                      ps = nps()
                      for k in range(2):
                          sc.mm(ps.all, kvn[k][:, blk * 128:(blk + 1) * 128], slot[:, k * 512:(k + 1) * 512],
                                start=(k == 0), stop=(k == 1))
                      sc.copy('act' if blk % 2 == 0 else 'dve', Vtok[:, 4 * j + blk, :], ps.all)

                  stage('B')
                  inv_sqrt = 1.0 / math.sqrt(192.0)
                  PT3 = [B16[5], B16[6], B16[11]]
                  psi['n'] = 4
                  for h in range(4):
                      o_ps = PS[4 + 2 * (h % 2)]; den_ps = PS[5 + 2 * (h % 2)]
                      nkb = 4 * j + 4
                      hp = h % 2

                      def qk(kb):
                          i = kb - 4 * j
                          q0 = 128 * i if i > 0 else 0
                          ksl = slice(kb * 128, (kb + 1) * 128)
                          s_ps = nps()
                          sc.mm(s_ps[:, q0:512], KnT[h][:, ksl], Qn[h][:, q0:512], start=True, stop=False)
                          sc.mm(s_ps[:, q0:512], KrT[64 * hp:64 * hp + 64, ksl], Qr[h // 2][64 * hp:64 * hp + 64, q0:512],
                                start=False, stop=True)
                          pt = PT3[kb % 3]
                          sc.act(pt[:, q0:512], s_ps[:, q0:512], AF.Exp, scale=inv_sqrt)
                          if i >= 0:
                              sc.tt('dve', pt[:, q0:q0 + 128], pt[:, q0:q0 + 128], dmask_bf, ALU.mult)

                      def pv(kb):
                          i = kb - 4 * j
                          q0 = 128 * i if i > 0 else 0
                          pt = PT3[kb % 3]
                          sc.mm(o_ps[:, q0:512], Vtok[:, kb, h * 128:(h + 1) * 128], pt[:, q0:512],
                                start=(kb == 0), stop=(kb == nkb - 1))
                          sc.mm(den_ps[:, q0:512], ones_bf, pt[:, q0:512], start=(kb == 0), stop=(kb == nkb - 1))

                      qk(0)
                      for kb in range(nkb):
                          if kb + 1 < nkb:
                              qk(kb + 1)
                          pv(kb)
                      sc.act(G[4][:, 0:512], den_ps.all, AF.Ln)
                      sc.act(G[4][:, 0:512], G[4][:, 0:512], AF.Exp, scale=-1.0)
                      sc.tt('dve', G[5][:, 0:512], o_ps.all, G[4][:, 0:512], ALU.mult)
                      sc.tt('dve', ymix[4 + h].all, G[5][:, 0:512], SG[h].all, ALU.mult)

                  psi['n'] = 6
                  stage('C')
                  for c in range(2):
                      base = 4 * c
                      ps = proj(('win', (l, base + 0)), 8, urhs, nps())
                      sc.copy('act', G[0][:, 0:512], ps.all)
                      ps = proj(('win', (l, base + 1)), 8, urhs, nps())
                      sc.tt('dve', ucv[c][:, 2:514], ps.all, G[0][:, 0:512], ALU.mult)
                      sc.ts('dve', G[1][:, 0:512], ucv[c][:, 2:514], pc(PP_CW + 2 * 2 + c), ALU.mult)
                      sc.stt(G[1][:, 0:512], ucv[c][:, 1:513], pc(PP_CW + 1 * 2 + c), G[1][:, 0:512], ALU.mult, ALU.add)
                      sc.stt(G[1][:, 0:512], ucv[c][:, 0:512], pc(PP_CW + 0 * 2 + c), G[1][:, 0:512], ALU.mult, ALU.add)
                      sc.copy('act', ucv[c][:, 0:2], ucv[c][:, 512:514])
                      ps = proj(('win', (l, base + 2)), 8, urhs, nps())
                      sc.tt('dve', G[2][:, 0:512], ps.all, G[1][:, 0:512], ALU.mult)
                      ps = proj(('win', (l, base + 3)), 8, urhs, nps())
                      sc.act(G[3][:, 0:512], ps.all, AF.Silu)
                      sc.tt('dve', ymix[c].all, G[2][:, 0:512], G[3][:, 0:512], ALU.mult)

                  stage('D')
                  WR = 8
                  Gw = G[7]
                  SGr = [B16[7], B16[8]]

                  def shifted(ps, idx, dst):
                      z = zraw[idx % 2]
                      sc.copy('act', z[:, 1:513], ps.all)
                      sc.copy('act', z[:, 0:1], zhalo[:, idx:idx + 1])
                      sc.copy('act', zhalo[:, idx:idx + 1], z[:, 512:513])
                      sc.tt('dve', G[0][:, 0:512], z[:, 0:512], z[:, 1:513], ALU.subtract)
                      sc.stt(dst, G[0][:, 0:512], pc(PP_MU + idx), z[:, 1:513], ALU.mult, ALU.add)

                  for idx in range(6):
                      ps = proj(('win', (l, WR + idx)), 8, urhs, nps())
                      shifted(ps, idx, X[idx].all)
                  ps = proj(('win', (l, WR + 6)), 8, urhs, nps())
                  shifted(ps, 6, Gw[:, 0:512])
                  for c in range(2):
                      ps = proj(('win', (l, WR + 7 + c)), 8, urhs, nps())
                      sc.act(SGr[c].all, ps.all, AF.Silu)
                  TW = B16[9]
                  sc.act(TW[0:64, :], Gw[0:64, 0:512], AF.Tanh)
                  sc.copy('dve', TW[64:128, :], Gw[64:128, 0:512])

                  stage('E1')
                  def gen_prep(fc):
                      rX, kX, vX = X[fc], X[2 + fc], X[4 + fc]
                      fsl = slice(fc * 128, (fc + 1) * 128)
                      if fc == 0:
                          lw, cl, eig, egx, aT, kx = [G[i][:, 0:512] for i in (0, 1, 3, 4, 5, 6)]
                      else:
                          lw, cl, eig, egx, aT, kx = cosT.all, sinT.all, G[7][:, 0:512], X[6].all, XT[0].all, XT[1].all
                      rs = BV[fc].all
                      sq = B16[fc]
                      prb = B16[5 + fc]
                      eg = EG[fc]
                      AR = ARs[fc]
                      ps_d = nps()
                      sc.mm(ps_d.all, w2a2[0:64, fsl], TW[0:64, :])
                      sc.act(lw, ps_d.all, AF.Sigmoid, bias=pc(PP_W0 + fc))
                      yield
                      ps_a = nps()
                      sc.mm(ps_a.all, w2a2[64:128, fsl], TW[64:128, :])
                      sc.act(aT, ps_a.all, AF.Sigmoid, bias=pc(PP_A0 + fc))
                      yield
                      sc.scan(cl, scanm, lw, 0.0, ALU.mult, ALU.add)
                      yield
                      sc.tt('dve', lw, cl, lw, ALU.subtract)
                      sc.act(eg.all, cl, AF.Exp, scale=-DECAY_SCALE)
                      yield
                      sc.act(eig, cl, AF.Exp, scale=DECAY_SCALE)
                      sc.ts('dve', kx, kX.all, pc(PP_KK + fc), ALU.mult)
                      yield
                      sc.act(egx, lw, AF.Exp, scale=-DECAY_SCALE)
                      yield
                      sc.act(sq.all, kx, AF.Square)
                      ps_s = nps()
                      sc.mm(ps_s.all, bones_bf, sq.all)
                      sc.act(rs, ps_s.all, AF.Ln, bias=eps_ap[1e-12], scale=1.0)
                      yield
                      sc.act(rs, rs, AF.Exp, scale=-0.5)
                      kp = lw
                      sc.ts('dve', kp, aT, -1.0, ALU.add, pc(PP_KA + fc), ALU.mult)
                      yield
                      sc.stt(kp, kp, 1.0, kX.all, ALU.add, ALU.mult)
                      yield
                      sc.tt('dve', kx, kx, rs, ALU.mult)
                      yield
                      BT, KT, BgT, KgT, VbT = RB[fc]
                      sc.stt(AR[:, 0, :], kx, -1.0, egx, ALU.mult, ALU.mult)
                      yield
                      sc.tt('dve', AR[:, 1, :], rX.all, eg.all, ALU.mult)
                      sc.copy('act', VbT.all, vX.all)
                      yield
                      bb = egx
                      sc.tt('dve', bb, kx, aT, ALU.mult)
                      yield
                      sc.tt('dve', BT.all, bb, eig, ALU.mult)
                      yield
                      sc.tt('dve', KT.all, kp, eig, ALU.mult)
                      yield
                      ratio = eig
                      sc.tt('dve', ratio.rr("p (c t) -> p c t", t=64), eig.rr("p (c t) -> p c t", t=64),
                            eg.all.rr("p (c t) -> p c t", t=64)[:, :, 63:64].bc([128, 8, 64]), ALU.mult)
                      yield
                      sc.tt('dve', BgT.all, bb, ratio, ALU.mult)
                      yield
                      sc.tt('dve', KgT.all, kp, ratio, ALU.mult)
                      yield
                      sc.stt(prb.all, rX.all, pc(PP_RK + fc), kp, ALU.mult, ALU.mult)
                      ps_b = nps()
                      sc.mm(ps_b.all, bones_bf, prb.all)
                      sc.tt('dve', BV[fc].all, ps_b.all, vX.all, ALU.mult)
                      yield
                      for blk in range(4):
                          bsl = slice(blk * 128, (blk + 1) * 128)
                          pst = nps()
                          pstb = pst.all.bitcast(BF16)
                          for qi, q in enumerate((AR[:, 0, bsl], BgT[:, bsl], KgT[:, bsl], VbT[:, bsl])):
                              sc.transpose(pstb[:, qi * 128:(qi + 1) * 128], q, ident_bf)
                          sc.copy('act' if blk % 2 else 'dve', tokms[fc][blk].all.rr("p q f -> p (q f)"), pstb[:, 0:512])
                          yield

                  _g = [gen_prep(0), gen_prep(1)]
                  _alive = [True, True]
                  while any(_alive):
                      for _i in range(2):
                          if _alive[_i]:
                              try:
                                  next(_g[_i])
                              except StopIteration:
                                  _alive[_i] = False

                  ypss = [PS[6], PS[7]]
                  psi['n'] = 6
                  for blk in range(4):
                      bsl = slice(blk * 128, (blk + 1) * 128)
                      st = [dict(lc=0, ncur=-1, pcur=0) for _ in range(2)]
                      for fc in range(2):
                          BT, KT, BgT, KgT, VbT = RB[fc]
                          AR = ARs[fc]; C = CM[fc]
                          psNB = nps(); psKB = nps(); psL = nps()
                          for hl in range(2):
                              rows = slice(64 * hl, 64 * hl + 64)
                              sc.mm(psNB[:, hl * 256:(hl + 1) * 256], BT[rows, bsl], AR[rows, :, bsl])
                              sc.mm(psKB[:, hl * 256:(hl + 1) * 256], KT[rows, bsl], AR[rows, :, bsl])
                              sc.mm(psL[:, hl * 128:(hl + 1) * 128], AR[rows, 0, bsl], BT[rows, bsl])
                          sc.tt('dve', C['NBm'].all, psNB.all.rr("p (h c) -> p h c", h=2),
                                mask2_bf.rr("p (o c) -> p o c", o=1).bc([128, 2, 256]), ALU.mult)
                          sc.tt('dve', C['KBm'].all, psKB.all.rr("p (h c) -> p h c", h=2),
                                mask2_bf.rr("p (o c) -> p o c", o=1).bc([128, 2, 256]), ALU.mult)
                          sc.tt('dve', C['Lm'][0].all, psL[:, 0:256].rr("p (h c) -> p h c", h=2),
                                maskTS_bf.rr("p (o c) -> p o c", o=1).bc([128, 2, 128]), ALU.mult)
                          sc.tt('dve', C['Pm'][0].all, C['NBm'][:, :, 0:128],
                                ident_bf.rr("p (o c) -> p o c", o=1).bc([128, 2, 128]), ALU.add)
                      for lev in range(1, 6):
                          for fc in range(2):
                              C = CM[fc]; s_ = st[fc]
                              lc, ncur = s_['lc'], s_['ncur']

                              def Nv(hl, C=C, ncur=ncur):
                                  return C['NBm'][:, hl, 0:128] if ncur < 0 else C['Nm'][ncur][:, hl, :]
                              psL2 = nps()
                              for hl in range(2):
                                  sc.mm(psL2[:, hl * 128:(hl + 1) * 128], Nv(hl), C['Lm'][lc][:, hl, :])
                              sc.copy('act', C['Lm'][1 - lc].all.rr("p h c -> p (h c)"), psL2[:, 0:256])
                              if lev < 5:
                                  psN2 = nps()
                                  for hl in range(2):
                                      sc.mm(psN2[:, hl * 128:(hl + 1) * 128], C['Lm'][lc][:, hl, :], Nv(hl))
                                  nnew = 0 if ncur < 0 else 1 - ncur
                                  sc.copy('dve', C['Nm'][nnew].all.rr("p h c -> p (h c)"), psN2[:, 0:256])
                                  s_['ncur'] = nnew
                              s_['lc'] = 1 - lc
                          for fc in range(2):
                              C = CM[fc]; s_ = st[fc]
                              lc, pcur = s_['lc'], s_['pcur']
                              psP = nps()
                              for hl in range(2):
                                  sc.mm(psP[:, hl * 128:(hl + 1) * 128], C['Lm'][lc][:, hl, :], C['Pm'][pcur][:, hl, :])
                              sc.tt('dve', C['Pm'][1 - pcur].all.rr("p h c -> p (h c)"), psP[:, 0:256],
                                    C['Pm'][pcur].all.rr("p h c -> p (h c)"), ALU.add)
                              s_['pcur'] = 1 - pcur
                      for fc in range(2):
                          C = CM[fc]; tk = tokms[fc][blk]; Tt = C['Pm'][st[fc]['pcur']]
                          psW = nps()
                          for hl in range(2):
                              sc.mm(psW[:, hl * 64:(hl + 1) * 64], C['KBm'][:, hl, 0:128], tk[:, 3, hl * 64:(hl + 1) * 64])
                          sc.copy('act', C['W1b'].all.rr("p h c -> p (h c)"), psW[:, 0:128])
                          psAh = nps()
                          for hl in range(2):
                              rows = slice(64 * hl, 64 * hl + 64)
                              sc.mm(psAh[rows, 0:128], tk[:, 0, hl * 64:(hl + 1) * 64], Tt[:, hl, :])
                          sc.copy('dve', C['AhT'].all, psAh[:, 0:128])
                      for fc in range(2):
                          C = CM[fc]; Tt = C['Pm'][st[fc]['pcur']]
                          psU0 = nps()
                          for hl in range(2):
                              sc.mm(psU0[:, hl * 64:(hl + 1) * 64], Tt[:, hl, :], C['W1b'][:, hl, :])
                          sc.copy('act', C['U0f'].all.rr("p h c -> p (h c)"), psU0[:, 0:128])
                      for ci in range(2):
                          cr = slice(64 * ci, 64 * ci + 64)
                          tcol = slice(blk * 128 + ci * 64, blk * 128 + ci * 64 + 64)
                          gcol = blk * 128 + ci * 64 + 63
                          hbo = [Hb[fc][hcur[fc]] for fc in range(2)]; hfo = [Hf[fc][hcur[fc]] for fc in range(2)]
                          hbn = [Hb[fc][1 - hcur[fc]] for fc in range(2)]; hfn = [Hf[fc][1 - hcur[fc]] for fc in range(2)]
                          for fc in range(2):
                              C = CM[fc]
                              psUc = nps()
                              sc.mm(psUc[cr, 0:128], C['AhT'][:, ci * 64:(ci + 1) * 64], hbo[fc].all)
                              sc.tt('dve', C['Ub'][cr, :, :].rr("p h c -> p (h c)"), psUc[cr, 0:128],
                                    C['U0f'][cr, :, :].rr("p h c -> p (h c)"), ALU.add)
                          psHs = []
                          for fc in range(2):
                              C = CM[fc]; tk = tokms[fc][blk]; AR = ARs[fc]; yps = ypss[fc]
                              sc.mm(yps[:, tcol], hbo[fc].all, AR[:, 1, tcol], start=True, stop=False)
                              for hl in range(2):
                                  rows = slice(64 * hl, 64 * hl + 64)
                                  sc.mm(yps[rows, tcol], C['Ub'][cr, hl, :], C['NBm'][cr, hl, 128 + ci * 64:128 + ci * 64 + 64],
                                        start=False, stop=False)
                                  sc.mm(yps[rows, tcol], tk[cr, 3, hl * 64:(hl + 1) * 64], C['KBm'][cr, hl, 128 + ci * 64:128 + ci * 64 + 64],
                                        start=False, stop=True)
                              psH = nps()
                              psHs.append(psH)
                              for hl in range(2):
                                  rows = slice(64 * hl, 64 * hl + 64)
                                  sc.mm(psH[rows, 0:64], tk[cr, 1, hl * 64:(hl + 1) * 64], C['Ub'][cr, hl, :], start=True, stop=False)
                                  sc.mm(psH[rows, 0:64], tk[cr, 2, hl * 64:(hl + 1) * 64], tk[cr, 3, hl * 64:(hl + 1) * 64],
                                        start=False, stop=True)
                          for fc in range(2):
                              psH = psHs[fc]; eg = EG[fc]
                              for hl in range(2):
                                  rows = slice(64 * hl, 64 * hl + 64)
                                  sc.stt(hbn[fc][rows, hl * 64:(hl + 1) * 64], hfo[fc][rows, :], eg[rows, gcol:gcol + 1], psH[rows, 0:64],
                                         ALU.mult, ALU.add)
                              sc.stt(hfn[fc].all, hfo[fc].all, eg[:, gcol:gcol + 1], psH[:, 0:64], ALU.mult, ALU.add)
                              hcur[fc] = 1 - hcur[fc]

                  yfs = [G[0], G[2]]; dds = [G[1], G[3]]; ybs = [B16[0], B16[1]]; rss = [rstd, G[4]]
                  for fc in range(2):
                      sc.copy('act', yfs[fc][:, 0:512], ypss[fc].all)
                      sc.copy('dve', ybs[fc].all, ypss[fc].all)
                  ps_ms = []
                  for fc in range(2):
                      ps_m = nps(); ps_ms.append(ps_m)
                      sc.mm(ps_m.all, bones_bf, ybs[fc].all)
                  for fc in range(2):
                      sc.stt(dds[fc][:, 0:512], ps_ms[fc].all, -1.0 / 64, yfs[fc][:, 0:512], ALU.mult, ALU.add)
                      sc.act(ybs[fc].all, dds[fc][:, 0:512], AF.Square)
                  ps_vs = []
                  for fc in range(2):
                      ps_v = nps(); ps_vs.append(ps_v)
                      sc.mm(ps_v.all, bones_bf, ybs[fc].all)
                  for fc in range(2):
                      sc.act(rss[fc][:, 0:512], ps_vs[fc].all, AF.Ln, bias=eps_ap[GN_EPS], scale=1.0 / 64)
                  for fc in range(2):
                      sc.act(rss[fc][:, 0:512], rss[fc][:, 0:512], AF.Exp, scale=-0.5)
                  for fc in range(2):
                      dd = dds[fc]
                      sc.tt('dve', dd[:, 0:512], dd[:, 0:512], rss[fc][:, 0:512], ALU.mult)
                      sc.ts('dve', dd[:, 0:512], dd[:, 0:512], pc(PP_GNW + fc), ALU.mult, pc(PP_GNB + fc), ALU.add)
                      sc.tt('dve', dd[:, 0:512], dd[:, 0:512], BV[fc].all, ALU.add)
                      sc.tt('dve', ymix[2 + fc].all, dd[:, 0:512], SGr[fc].all, ALU.mult)

                  stage('E')
                  for i in range(8):
                      sc.dma('sp', G[i][:, 0:512], src[i * 128:(i + 1) * 128, tsl])
                  if j + 1 < NT:
                      nsl = slice(t0 + TT, t0 + 2 * TT)
                      for i in range(8):
                          sc.dma('sp', HN[i].all, src[i * 128:(i + 1) * 128, nsl])
                  yrhs = [ymix[i].all for i in range(8)]
                  for oc in range(8):
                      ps = proj(('wout', (l, oc)), 8, yrhs, nps())
                      sc.tt('dve', G[oc][:, 0:512], ps.all, G[oc][:, 0:512], ALU.add)

                  stage('F')
                  for k in range(2):
                      sc.dma('pool', PB[k].all, pT[l, k * 128:(k + 1) * 128, tsl])
                  psA = nps()
                  for i in range(8):
                      sq = B16[i % 2]
                      sc.act(sq.all, G[i][:, 0:512], AF.Square)
                      sc.mm(psA.all, ones_bf, sq.all, start=(i == 0), stop=(i == 7))
                  rsqrt_bc(rstd.all, psA.all, 1.0 / D_MODEL, NORM_EPS)
                  for i in range(8):
                      sc.stt(uT[i].all, G[i][:, 0:512], pc(PP_G2 + i), rstd.all, ALU.mult, ALU.mult)
                  prhs = [PB[k].all for k in range(2)]
                  for oc in range(8):
                      ps_g = proj(('wgate', (l, oc)), 8, urhs, nps())
                      sg = [cosT, sinT][oc % 2]
                      sc.act(sg.all, ps_g.all, AF.Sigmoid)
                      ps_p = proj(('wple', (l, oc)), 2, prhs, nps())
                      sc.tt('dve', sg.all, ps_p.all, sg.all, ALU.mult)
                      sc.tt('dve', G[oc][:, 0:512], sg.all, G[oc][:, 0:512], ALU.add)

                  if l < L - 1:
                      for i in range(8):
                          sc.dma('sp', hscr[i * 128:(i + 1) * 128, tsl], G[i][:, 0:512])
                  else:
                      psA = nps()
                      for i in range(8):
                          sq = B16[i % 2]
                          sc.act(sq.all, G[i][:, 0:512], AF.Square)
                          sc.mm(psA.all, ones_bf, sq.all, start=(i == 0), stop=(i == 7))
                      rsqrt_bc(rstd.all, psA.all, 1.0 / D_MODEL, NORM_EPS)
                      for i in range(8):
                          sc.stt(G[i][:, 0:512], G[i][:, 0:512], fg[:, i:i + 1], rstd.all, ALU.mult, ALU.mult)
                          sc.dma('sp', outT[i * 128:(i + 1) * 128, tsl], G[i][:, 0:512])

        except _StopBuild:
            pass
        sc.emit(final_wait_bufs=[outT])
        build_program.last_stats = dict(sbuf=sc.sbuf_bytes, counts=dict(sc.count))
    return nc


def _const_matrix():
    cm = np.zeros((128, NCM), np.float32)
    cm[:, CM_ONES:CM_ONES + 128] = 1.0
    idx = np.arange(128)
    same = (idx[:, None] // 64) == (idx[None, :] // 64)
    cm[:, CM_BONES:CM_BONES + 128] = same
    cm[:, CM_IDENT:CM_IDENT + 128] = np.eye(128)
    s = idx[:, None]; t = idx[None, :]
    cm[:, CM_MASK2:CM_MASK2 + 128] = same & (t > s)
    cm[:, CM_MASK2 + 128:CM_MASK2 + 256] = same & (t >= s)
    cm[:, CM_MASKTS:CM_MASKTS + 128] = same & (idx[None, :] < idx[:, None])
    cm[:, CM_DMASK:CM_DMASK + 128] = (idx[:, None] // 64) <= (idx[None, :] // 64)
    sm = np.ones(512, np.float32); sm[::64] = 0.0
    cm[:, CM_SCAN:CM_SCAN + 512] = sm[None, :]
    inv_freq = (1.0 / (np.float32(ROPE_THETA) ** (np.arange(0, 64, 2, dtype=np.float32) / np.float32(64)))).astype(np.float32)
    cm[:, CM_INVF] = inv_freq[idx % 32]
    cm[:, CM_SSIGN] = np.where((idx % 64) < 32, -1.0, 1.0)
    return cm


def _chunk_cols(w, cols, K):
    sub = w[:, cols]
    return np.ascontiguousarray(sub.reshape(K, 128, 128).transpose(1, 0, 2).reshape(128, K * 128))


def _layout_weights(inp, L):
    A = np.arange
    o_cb, o_cc, o_ch, o_cg = 0, 256, 512, 768
    o_r = 1024; o_k = 1280; o_v = 1536; o_wd = 1792; o_ad = 1856; o_rg = 1920
    o_qa = 2176; o_kva = 2560; o_kr = 2816; o_mg = 2880
    chunks = []
    for c in range(2):
        for o in (o_cc, o_ch, o_cb, o_cg):
            chunks.append(o + c * 128 + A(128))
    for o in (o_r, o_r + 128, o_k, o_k + 128, o_v, o_v + 128):
        chunks.append(o + A(128))
    chunks.append(np.concatenate([o_wd + A(64), o_ad + A(64)]))
    chunks.append(o_rg + A(128)); chunks.append(o_rg + 128 + A(128))
    for c in range(3):
        chunks.append(o_qa + c * 128 + A(128))
    for c in range(2):
        chunks.append(o_kva + c * 128 + A(128))
    kr = o_kr + A(64)
    ksw = o_kr + np.concatenate([32 + A(32), A(32)])
    chunks.append(np.concatenate([kr, kr])); chunks.append(np.concatenate([ksw, ksw]))
    for c in range(4):
        chunks.append(o_mg + c * 128 + A(128))
    assert len(chunks) == NWCH
    win = np.stack([np.stack([_chunk_cols(inp["w_in"][l], ch, 8) for ch in chunks]) for l in range(L)])
    w2a2 = np.stack([np.concatenate([inp["rwkv_w2"][l], inp["rwkv_a2"][l]], axis=0) for l in range(L)])
    qch = []
    for h in range(4):
        qch.append(h * 192 + A(128))
    rope = [h * 192 + 128 + A(64) for h in range(4)]
    ropesw = [h * 192 + 128 + np.concatenate([32 + A(32), A(32)]) for h in range(4)]
    qch.append(np.concatenate([rope[0], rope[1]])); qch.append(np.concatenate([rope[2], rope[3]]))
    qch.append(np.concatenate([ropesw[0], ropesw[1]])); qch.append(np.concatenate([ropesw[2], ropesw[3]]))
    wqb = np.stack([np.stack([_chunk_cols(inp["mla_w_qb"][l], ch, 3) for ch in qch]) for l in range(L)])
    wkvk = np.stack([np.stack([_chunk_cols(inp["mla_w_kvb"][l], h * 256 + A(128), 2) for h in range(4)]) for l in range(L)])
    vcols = np.concatenate([h * 256 + 128 + A(128) for h in range(4)])
    wkvv = np.stack([np.ascontiguousarray(
        inp["mla_w_kvb"][l][:, vcols].reshape(2, 128, 512).transpose(1, 0, 2).reshape(128, 1024)) for l in range(L)])
    wout = np.stack([np.stack([_chunk_cols(inp["w_out"][l], oc * 128 + A(128), 8) for oc in range(8)]) for l in range(L)])
    wgate = np.stack([np.stack([_chunk_cols(inp["ple_gate_w"][l], oc * 128 + A(128), 8) for oc in range(8)]) for l in range(L)])
    wple = np.stack([np.stack([_chunk_cols(inp["ple_w"][l], oc * 128 + A(128), 2) for oc in range(8)]) for l in range(L)])
    pp = np.zeros((L, 128, NPP), np.float32)
    for l in range(L):
        cols = []
        cols += [inp["norm_mix_g"][l][i * 128:(i + 1) * 128] for i in range(8)]
        cols += [inp["ple_norm_g"][l][i * 128:(i + 1) * 128] for i in range(8)]
        for tap in range(3):
            for c in range(2):
                cols.append(inp["conv_w"][l][tap, c * 128:(c + 1) * 128])
        mu = inp["rwkv_mu"][l]
        for i in range(6):
            cols.append(mu[i * 128:(i + 1) * 128])
        cols.append(mu[768:896])
        for nm in ("rwkv_w0", "rwkv_a0", "rwkv_kk", "rwkv_ka"):
            for c in range(2):
                cols.append(inp[nm][l][c * 128:(c + 1) * 128])
        rk = inp["rwkv_rk"][l].reshape(256)
        for c in range(2):
            cols.append(rk[c * 128:(c + 1) * 128])
        for nm in ("rwkv_gn_w", "rwkv_gn_b"):
            for c in range(2):
                cols.append(inp[nm][l][c * 128:(c + 1) * 128])
        for c in range(3):
            cols.append(inp["mla_q_norm_g"][l][c * 128:(c + 1) * 128])
        for c in range(2):
            cols.append(inp["mla_kv_norm_g"][l][c * 128:(c + 1) * 128])
        assert len(cols) == NPP
        pp[l] = np.stack(cols, axis=1)
    fg = np.ascontiguousarray(inp["final_norm_g"].reshape(8, 128).T)
    return dict(win=win, w2a2=w2a2, wqb=wqb, wkvk=wkvk, wkvv=wkvv, wout=wout, wgate=wgate, wple=wple, pp=pp, fg=fg)


_PROG_CACHE = {}


def run_cores(inp, S, L, nb, n_cores):
    inp = {k: np.asarray(v) for k, v in inp.items()}
    wl = _layout_weights(inp, L)
    cmat = _const_matrix()
    key = (S, L)
    if key not in _PROG_CACHE:
        build_program(S, L)
        _PROG_CACHE[key] = build_program(S, L, wseq=list(build_program.wrec))
    nc = _PROG_CACHE[key]
    in_maps = []
    for c in range(n_cores):
        b = c % nb
        m = dict(wl)
        m["cmat"] = cmat
        m["xT"] = np.ascontiguousarray(inp["x"][b].T)
        m["pT"] = np.ascontiguousarray(np.transpose(inp["p"][:, b], (0, 2, 1)))
        m["pos"] = np.ascontiguousarray(np.broadcast_to(inp["positions"][b][None, :].astype(np.int32), (128, S)))
        in_maps.append(m)
    res = run_bass_kernel_spmd(nc, in_maps, core_ids=list(range(n_cores)))
    out = np.stack([np.ascontiguousarray(res.results[b]["outT"].T) for b in range(nb)])
    return out.astype(np.float32)


def kernel(**inputs):
    return run_cores(inputs, SEQ, DEPTH, BATCH, 8)
```

```python
import math
import contextlib
import numpy as np
import concourse.bass as bass
import concourse.mybir as mybir
from concourse.bass_utils import run_bass_kernel_spmd

F32 = mybir.dt.float32
BF16 = mybir.dt.bfloat16
I32 = mybir.dt.int32
AF = mybir.ActivationFunctionType
ALU = mybir.AluOpType

SAME_ENGINE_SYNC = False
SERIALIZE_ROW_SWITCH = True

D_MODEL = 1024
DEPTH = 2
BATCH = 4
SEQ = 4096
TT = 512
NORM_EPS = 1e-6
GN_EPS = 64e-5
DECAY_SCALE = math.exp(-0.5)
ROPE_THETA = 10000.0
NWCH = 28


class Tn:
    def __init__(self, h, name, dsem=None):
        self.h = h; self.name = name; self.lastw = None; self.readers = {}; self.dsem = dsem

    def __getitem__(self, idx):
        return V(self.h[idx], self)

    @property
    def all(self):
        return V(self.h.ap(), self)


class V:
    def __init__(self, ap, buf):
        self.ap = ap; self.buf = buf

    def bc(self, shape):
        return V(self.ap.broadcast_to(list(shape)), self.buf)

    def bitcast(self, dt):
        return V(self.ap.bitcast(dt), self.buf)

    def __getitem__(self, idx):
        return V(self.ap[idx], self.buf)

    def rr(self, s, **kw):
        return V(self.ap.rearrange(s, **kw), self.buf)


class Sched:
    def __init__(self, nc, stack):
        self.nc = nc; self.stack = stack
        self.streams = {e: [] for e in ('pe', 'act', 'dve', 'pool', 'sp')}
        self.count = {e: 0 for e in ('pe', 'act', 'dve', 'pool')}
        self.dcount = {}
        self.known = {e: {} for e in self.streams}
        self.ndsem = 0
        self.sbuf_bytes = 0

    def new_dsem(self):
        self.ndsem += 1
        k = 'dma%d' % self.ndsem
        self.dcount[k] = 0
        return k

    def sbuf(self, name, shape, dt, dsem=False):
        h = self.stack.enter_context(self.nc.sbuf_tensor("s_" + name, list(shape), dt))
        n = 1
        for s in shape[1:]:
            n *= s
        self.sbuf_bytes += n * (4 if dt in (F32, I32) else 2)
        return Tn(h, name, self.new_dsem() if dsem else None)

    def psum(self, name, shape, dt):
        h = self.stack.enter_context(self.nc.psum_tensor(name, list(shape), dt))
        return Tn(h, name)

    def dram(self, name, shape, dt, kind):
        h = self.nc.dram_tensor(name, list(shape), dt, kind=kind)
        return Tn(h, name)

    def _need(self, eng, tok, waits):
        if tok is None:
            return
        key, val = tok
        if key == eng and (eng == 'pe' or not SAME_ENGINE_SYNC):
            return
        if key in self.dcount:
            val = max(val, self.dcount[key])
        if self.known[eng].get(key, 0) >= val:
            return
        self.known[eng][key] = val
        waits.append((key, val))

    def op(self, eng, fn, reads=(), writes=(), dma=None, extra_waits=()):
        waits = [tuple(w) for w in extra_waits]
        rb = []
        for v in reads:
            b = v.buf if isinstance(v, V) else v
            if b not in rb:
                rb.append(b)
        wb = []
        for v in writes:
            b = v.buf if isinstance(v, V) else v
            if b not in wb:
                wb.append(b)
        for b in rb:
            self._need(eng, b.lastw, waits)
        for b in wb:
            self._need(eng, b.lastw, waits)
            for k, val in b.readers.items():
                self._need(eng, (k, val), waits)
        if dma is None:
            self.count[eng] += 1
            tok = (eng, self.count[eng])
        else:
            self.dcount[dma] += 16
            tok = (dma, self.dcount[dma])
        for b in rb:
            if b in wb:
                continue
            if b.readers.get(tok[0], 0) < tok[1]:
                b.readers[tok[0]] = tok[1]
        for b in wb:
            b.lastw = tok
            b.readers = {}
        self.streams[eng].append((waits, fn, tok, dma is not None))
        return tok

    def dma(self, eng, out, in_, **kw):
        sem = out.buf.dsem if out.buf.dsem is not None else in_.buf.dsem
        assert sem is not None, (out.buf.name, in_.buf.name)
        o, i = out.ap, in_.ap
        return self.op(eng, lambda e: e.dma_start(out=o, in_=i, **kw), reads=[in_], writes=[out], dma=sem)

    def mm(self, out, lhsT, rhs, start=True, stop=True, **kw):
        o, l, r = out.ap, lhsT.ap, rhs.ap
        r0 = int(l.start_partition()); r1 = r0 + int(l.partition_size())
        prev = getattr(self, '_prev_mm', None)
        extra = []
        if prev is not None:
            (p0, p1, pbank, ptok) = prev
            if (r1 <= p0 or p1 <= r0) and (pbank is out.buf or SERIALIZE_ROW_SWITCH):
                extra.append(ptok)
                self.n_serial = getattr(self, 'n_serial', 0) + 1
        tok = self.op('pe', lambda e: e.matmul(o, l, r, start=start, stop=stop, **kw),
                      reads=[lhsT, rhs], writes=[out], extra_waits=extra)
        self._prev_mm = (r0, r1, out.buf, tok)
        return tok

    def transpose(self, out, in_, ident):
        o, i, d = out.ap, in_.ap, ident.ap
        self._prev_mm = None
        return self.op('pe', lambda e: e.transpose(o, i, d), reads=[in_, ident], writes=[out])

    def act(self, out, in_, func, bias=None, scale=None):
        o, i = out.ap, in_.ap
        kw = {}
        rd = [in_]
        if bias is not None:
            if isinstance(bias, V):
                kw['bias'] = bias.ap; rd.append(bias)
            else:
                kw['bias'] = bias
        if scale is not None:
            if isinstance(scale, V):
                kw['scale'] = scale.ap; rd.append(scale)
            else:
                kw['scale'] = scale
        return self.op('act', lambda e: e.activation(o, i, func, **kw), reads=rd, writes=[out])

    def tt(self, eng, out, in0, in1, op):
        o, a, b = out.ap, in0.ap, in1.ap
        return self.op(eng, lambda e: e.tensor_tensor(o, a, b, op), reads=[in0, in1], writes=[out])

    def ts(self, eng, out, in0, s1, op0, s2=None, op1=None):
        o, a = out.ap, in0.ap
        rd = [in0]
        if isinstance(s1, V):
            rd.append(s1); s1 = s1.ap
        if isinstance(s2, V):
            rd.append(s2); s2 = s2.ap
        if op1 is None:
            return self.op(eng, lambda e: e.tensor_scalar(o, a, s1, None, op0), reads=rd, writes=[out])
        return self.op(eng, lambda e: e.tensor_scalar(o, a, s1, s2, op0, op1), reads=rd, writes=[out])

    def stt(self, out, in0, scalar, in1, op0, op1):
        o, a, b = out.ap, in0.ap, in1.ap
        rd = [in0, in1]
        if isinstance(scalar, V):
            rd.append(scalar); scalar = scalar.ap
        return self.op('dve', lambda e: e.scalar_tensor_tensor(o, a, scalar, b, op0, op1), reads=rd, writes=[out])

    def copy(self, eng, out, in_):
        o, i = out.ap, in_.ap
        if eng == 'act':
            return self.op('act', lambda e: e.copy(o, i), reads=[in_], writes=[out])
        return self.op(eng, lambda e: e.tensor_copy(o, i), reads=[in_], writes=[out])

    def memset(self, eng, out, val):
        o = out.ap
        return self.op(eng, lambda e: e.memset(o, val), reads=[], writes=[out])

    def scan(self, out, d0, d1, init, op0, op1):
        o, a, b = out.ap, d0.ap, d1.ap
        return self.op('dve', lambda e: e.tensor_tensor_scan(o, a, b, init, op0, op1), reads=[d0, d1], writes=[out])

    def recip(self, out, in_):
        o, i = out.ap, in_.ap
        return self.op('dve', lambda e: e.reciprocal(o, i), reads=[in_], writes=[out])

    def emit(self, final_wait_bufs=()):
        nc = self.nc
        sems = {}
        for k in list(self.count.keys()) + list(self.dcount.keys()):
            sems[k] = self.stack.enter_context(nc.semaphore(k))
        finals = []
        for b in final_wait_bufs:
            if b.lastw is not None:
                finals.append(b.lastw)
        streams = self.streams

        def run(name, e):
            for waits, fn, tok, isdma in streams[name]:
                for key, val in waits:
                    e.wait_ge(sems[key], val)
                ins = fn(e)
                ins.then_inc(sems[tok[0]], 16 if isdma else 1)
            if name == 'sp':
                for key, val in finals:
                    e.wait_ge(sems[key], max(val, self.dcount.get(key, 0)))

        with nc.Block() as block:
            @block.tensor
            def _(e):
                run('pe', e)

            @block.scalar
            def _(e):
                run('act', e)

            @block.vector
            def _(e):
                run('dve', e)

            @block.gpsimd
            def _(e):
                run('pool', e)

            @block.sync
            def _(e):
                run('sp', e)
        for k, v in self.count.items():
            assert v < 60000, (k, v)
        for k, v in self.dcount.items():
            assert v < 60000, (k, v)


PP_G1 = 0
PP_G2 = 8
PP_CW = 16
PP_MU = 22
PP_W0 = 29
PP_A0 = 31
PP_KK = 33
PP_KA = 35
PP_RK = 37
PP_GNW = 39
PP_GNB = 41
PP_GQ = 43
PP_GKV = 46
NPP = 48

CM_ONES = 0
CM_BONES = 128
CM_IDENT = 256
CM_MASK2 = 384
CM_MASKTS = 640
CM_DMASK = 768
CM_SCAN = 896
CM_INVF = 1408
CM_SSIGN = 1409
NCM = 1410


class _StopBuild(Exception):
    pass


def build_program(S, L=DEPTH, dbg=None, wseq=None):
    NT = S // TT
    NB = S // 128
    nc = bass.Bass("TRN2", target_bir_lowering=False)
    st = contextlib.ExitStack()
    with st:
        sc = Sched(nc, st)
        xT = sc.dram("xT", [D_MODEL, S], F32, "ExternalInput")
        pT = sc.dram("pT", [L, 256, S], F32, "ExternalInput")
        posd = sc.dram("pos", [128, S], I32, "ExternalInput")
        ppd = sc.dram("pp", [L, 128, NPP], F32, "ExternalInput")
        fgd = sc.dram("fg", [128, 8], F32, "ExternalInput")
        cmd = sc.dram("cmat", [128, NCM], F32, "ExternalInput")
        wind = sc.dram("win", [L, NWCH, 128, 1024], F32, "ExternalInput")
        w2a2d = sc.dram("w2a2", [L, 128, 256], F32, "ExternalInput")
        wqbd = sc.dram("wqb", [L, 8, 128, 384], F32, "ExternalInput")
        wkvkd = sc.dram("wkvk", [L, 4, 128, 256], F32, "ExternalInput")
        wkvvd = sc.dram("wkvv", [L, 128, 1024], F32, "ExternalInput")
        woutd = sc.dram("wout", [L, 8, 128, 1024], F32, "ExternalInput")
        wgated = sc.dram("wgate", [L, 8, 128, 1024], F32, "ExternalInput")
        wpled = sc.dram("wple", [L, 8, 128, 256], F32, "ExternalInput")
        outT = sc.dram("outT", [D_MODEL, S], F32, "ExternalOutput")
        hscr = sc.dram("hscr", [D_MODEL, S], F32, "Internal")

        cm = sc.sbuf("cm", [128, NCM - 896], F32, dsem=True)
        cb = sc.sbuf("cb", [128, 896], BF16, dsem=True)
        pp = [sc.sbuf("pp%d" % l, [128, NPP], F32, dsem=True) for l in range(L)]
        fg = sc.sbuf("fg", [128, 8], F32, dsem=True)
        KnT = [sc.sbuf("KnT%d" % h, [128, S], BF16) for h in range(4)]
        KrT = sc.sbuf("KrT", [128, S], BF16)
        Vtok = sc.sbuf("Vtok", [128, NB, 512], BF16)
        NWB = 5
        wbuf = [sc.sbuf("wbuf%d" % i, [128, 1024], BF16, dsem=True) for i in range(NWB)]
        G = [sc.sbuf("G%d" % i, [128, 514], F32, dsem=True) for i in range(8)]
        X = [sc.sbuf("X%d" % i, [128, 512], F32, dsem=True) for i in range(7)]
        uT = [sc.sbuf("uT%d" % i, [128, 512], BF16) for i in range(8)]
        ymix = [sc.sbuf("ymix%d" % i, [128, 512], BF16) for i in range(8)]
        B16 = [sc.sbuf("B16_%d" % i, [128, 512], BF16) for i in range(12)]
        PB = [sc.sbuf("PB%d" % i, [128, 512], BF16, dsem=True) for i in range(2)]
        ARs = [sc.sbuf("AR%d" % f, [128, 2, 512], BF16) for f in range(2)]
        RB = [[sc.sbuf("RB%d_%d" % (f, i), [128, 512], BF16) for i in range(5)] for f in range(2)]
        EG = [sc.sbuf("EG%d" % f, [128, 512], F32, dsem=True) for f in range(2)]
        BV = [sc.sbuf("BV%d" % f, [128, 512], F32) for f in range(2)]
        rstd = sc.sbuf("rstd", [128, 512], F32)
        XT = [sc.sbuf("XT%d" % i, [128, 512], F32) for i in range(2)]
        cosT = sc.sbuf("cosT", [128, 512], F32)
        sinT = sc.sbuf("sinT", [128, 512], F32)
        ucv = [sc.sbuf("ucv%d" % c, [128, 514], F32) for c in range(2)]
        zraw = [G[1], G[2]]
        zhalo = sc.sbuf("zhalo", [128, 8], F32)
        w2a2 = sc.sbuf("w2a2", [128, 256], BF16, dsem=True)
        tokms = [[sc.sbuf("tokm%d_%d" % (f, b), [128, 4, 128], BF16) for b in range(4)] for f in range(2)]
        CM = []
        for f in range(2):
            CM.append(dict(
                NBm=sc.sbuf("NBm%d" % f, [128, 2, 256], BF16), KBm=sc.sbuf("KBm%d" % f, [128, 2, 256], BF16),
                Lm=[sc.sbuf("Lm%d_%d" % (f, i), [128, 2, 128], BF16) for i in range(2)],
                Nm=[sc.sbuf("Nm%d_%d" % (f, i), [128, 2, 128], BF16) for i in range(2)],
                Pm=[sc.sbuf("Pm%d_%d" % (f, i), [128, 2, 128], BF16) for i in range(2)],
                W1b=sc.sbuf("W1b%d" % f, [128, 2, 64], BF16), U0f=sc.sbuf("U0f%d" % f, [128, 2, 64], F32),
                Ub=sc.sbuf("Ub%d" % f, [128, 2, 64], BF16), AhT=sc.sbuf("AhT%d" % f, [128, 128], BF16)))
        Hf = [[sc.sbuf("Hf%d_%d" % (fc, i), [128, 64], F32) for i in range(2)] for fc in range(2)]
        Hb = [[sc.sbuf("Hb%d_%d" % (fc, i), [128, 128], BF16) for i in range(2)] for fc in range(2)]
        PS = [sc.psum("ps%d" % i, [128, 512], F32) for i in range(8)]
        dbgbuf = None

        ones_bf = cb[:, CM_ONES:CM_ONES + 128]
        bones_bf = cb[:, CM_BONES:CM_BONES + 128]
        ident_bf = cb[:, CM_IDENT:CM_IDENT + 128]
        mask2_bf = cb[:, CM_MASK2:CM_MASK2 + 256]
        maskTS_bf = cb[:, CM_MASKTS:CM_MASKTS + 128]
        dmask_bf = cb[:, CM_DMASK:CM_DMASK + 128]
        scanm = cm[:, CM_SCAN - 896:CM_SCAN - 896 + 512]
        invf = cm[:, CM_INVF - 896:CM_INVF - 896 + 1]
        ssign = cm[:, CM_SSIGN - 896:CM_SSIGN - 896 + 1]

        sc.dma('sp', cm.all, cmd[:, 896:NCM])
        sc.dma('pool', cb.all, cmd[:, 0:896])
        sc.dma('sp', fg.all, fgd.all)
        for l in range(L):
            sc.dma('sp', pp[l].all, ppd[l])

        wstate = {'i': 0, 'issued': 0}
        wrec = []
        build_program.wrec = wrec
        dramw = {'win': wind, 'wqb': wqbd, 'wkvk': wkvkd, 'wkvv': wkvvd, 'wout': woutd, 'wgate': wgated, 'wple': wpled}
        LOOKAHEAD = NWB - 1

        def _issue(k, key, ncols):
            name, idx = key
            sc.dma('pool', wbuf[k % NWB][:, 0:ncols], dramw[name][idx])

        def wload(key, ncols):
            i = wstate['i']
            wstate['i'] += 1
            wrec.append((key, ncols))
            if wseq is None:
                _issue(i, key, ncols)
            else:
                assert wseq[i] == (key, ncols), (i, wseq[i], key)
                hi = min(i + LOOKAHEAD, len(wseq) - 1)
                while wstate['issued'] <= hi:
                    k = wstate['issued']
                    _issue(k, wseq[k][0], wseq[k][1])
                    wstate['issued'] += 1
            return wbuf[i % NWB]

        def proj(key, K, rhs, ps):
            slot = wload(key, K * 128)
            for k in range(K):
                sc.mm(ps.all, slot[:, k * 128:(k + 1) * 128], rhs[k], start=(k == 0), stop=(k == K - 1))
            return ps

        psi = {'i': 0, 'n': 6}

        def nps():
            p = PS[psi['i'] % psi['n']]
            psi['i'] += 1
            return p

        def rsqrt_bc(out, ps, scale, eps):
            sc.act(out, ps, AF.Ln, bias=eps_ap[eps], scale=scale)
            sc.act(out, out, AF.Exp, scale=-0.5)

        cvals = sc.sbuf("cvals", [128, 8], F32)
        eps_ap = {}
        for i, v in enumerate([NORM_EPS, 1e-12, GN_EPS, math.pi / 2, 0.0]):
            sc.memset('dve', cvals[:, i:i + 1], float(v))
            eps_ap[v] = cvals[:, i:i + 1]

        marks = []
        build_program.marks = marks

        def stage(name):
            marks.append((name, sc.count['pe']))
            if dbg is not None and dbg == name:
                for i in range(8):
                    sc.dma('sp', outT[i * 128:(i + 1) * 128, 0:512], G[i][:, 0:512])
                raise _StopBuild()

        try:
          for l in range(L):
              src = xT if l == 0 else hscr
              P = pp[l]

              def pc(i):
                  return P[:, i:i + 1]

              sc.dma('pool', w2a2.all, w2a2d[l])
              for c in range(2):
                  sc.memset('dve', ucv[c][:, 0:2], 0.0)
              sc.memset('dve', zhalo.all, 0.0)
              hcur = [0, 0]
              for fc in range(2):
                  sc.memset('dve', Hf[fc][0].all, 0.0)
                  sc.memset('dve', Hb[fc][0].all, 0.0)
                  sc.memset('dve', Hb[fc][1].all, 0.0)

              for j in range(NT):
                  t0 = j * TT
                  tsl = slice(t0, t0 + TT)

                  HN = [X[0], X[1], X[2], X[3], X[4], X[5], EG[0], EG[1]]
                  if j > 0:
                      hsrc = [HN[i].all for i in range(8)]
                  else:
                      for i in range(8):
                          sc.dma('sp', G[i][:, 0:512], src[i * 128:(i + 1) * 128, tsl])
                      hsrc = [G[i][:, 0:512] for i in range(8)]
                  psA = nps()
                  for i in range(8):
                      sq = B16[i % 2]
                      sc.act(sq.all, hsrc[i], AF.Square)
                      sc.mm(psA.all, ones_bf, sq.all, start=(i == 0), stop=(i == 7))
                  rsqrt_bc(rstd.all, psA.all, 1.0 / D_MODEL, NORM_EPS)
                  for i in range(8):
                      sc.stt(uT[i].all, hsrc[i], pc(PP_G1 + i), rstd.all, ALU.mult, ALU.mult)
                  urhs = [uT[i].all for i in range(8)]

                  stage('A')
                  posiV = X[3].all.bitcast(I32)
                  sc.dma('sp', posiV, posd[:, tsl])
                  ang = X[6]
                  sc.copy('dve', ang.all, posiV)
                  sc.ts('dve', ang.all, ang.all, invf, ALU.mult)
                  for (dst, off) in ((sinT, 0.0), (cosT, 0.25)):
                      m = G[0]
                      sc.ts('dve', m[:, 0:512], ang.all, 1.0 / (2 * math.pi), ALU.mult, off, ALU.add)
                      sc.copy('dve', posiV, m[:, 0:512])
                      sc.copy('dve', m[:, 0:512], posiV)
                      C1 = 6.28125
                      C2 = 2 * math.pi - C1
                      sc.stt(G[1][:, 0:512], m[:, 0:512], -C1, ang.all, ALU.mult, ALU.add)
                      sc.stt(G[1][:, 0:512], m[:, 0:512], -C2, G[1][:, 0:512], ALU.mult, ALU.add)
                      PI_LO = 3.1415925
                      if off == 0.0:
                          sc.ts('dve', G[1][:, 0:512], G[1][:, 0:512], -PI_LO, ALU.max, PI_LO, ALU.min)
                          sc.act(dst.all, G[1][:, 0:512], AF.Sin)
                          sc.ts('dve', dst.all, dst.all, ssign, ALU.mult)
                      else:
                          sc.ts('dve', G[1][:, 0:512], G[1][:, 0:512], math.pi / 2, ALU.add, -PI_LO, ALU.max)
                          sc.ts('dve', G[1][:, 0:512], G[1][:, 0:512], PI_LO, ALU.min)
                          sc.act(dst.all, G[1][:, 0:512], AF.Sin)

                  def rope_combine(dst, ps1, ps2):
                      sc.tt('dve', G[2][:, 0:512], ps1.all, cosT.all, ALU.mult)
                      sc.tt('dve', G[3][:, 0:512], ps2.all, sinT.all, ALU.mult)
                      sc.tt('dve', dst, G[2][:, 0:512], G[3][:, 0:512], ALU.add)

                  stage('rot')
                  WQA, WKVA, WKR, WGM = 17, 20, 22, 24
                  sqq = [B16[0], B16[1], B16[11]]; sqk = [B16[9], B16[10]]
                  for c in range(3):
                      ps = proj(('win', (l, WQA + c)), 8, urhs, nps())
                      sc.copy('act', G[4 + c][:, 0:512], ps.all)
                      sc.act(sqq[c].all, G[4 + c][:, 0:512], AF.Square)
                  for c in range(2):
                      ps = proj(('win', (l, WKVA + c)), 8, urhs, nps())
                      sc.copy('act', G[c][:, 0:512], ps.all)
                      sc.act(sqk[c].all, G[c][:, 0:512], AF.Square)
                  psBq = nps()
                  for c in range(3):
                      sc.mm(psBq.all, ones_bf, sqq[c].all, start=(c == 0), stop=(c == 2))
                  psBk = nps()
                  for c in range(2):
                      sc.mm(psBk.all, ones_bf, sqk[c].all, start=(c == 0), stop=(c == 1))
                  rsqrt_bc(rstd.all, psBq.all, 1.0 / 384, NORM_EPS)
                  rsqrt_bc(G[7][:, 0:512], psBk.all, 1.0 / 256, NORM_EPS)
                  qn = [B16[2 + c] for c in range(3)]
                  for c in range(3):
                      sc.stt(qn[c].all, G[4 + c][:, 0:512], pc(PP_GQ + c), rstd.all, ALU.mult, ALU.mult)
                  kvn = [B16[5 + c] for c in range(2)]
                  for c in range(2):
                      sc.stt(kvn[c].all, G[c][:, 0:512], pc(PP_GKV + c), G[7][:, 0:512], ALU.mult, ALU.mult)
                  ps1 = proj(('win', (l, WKR)), 8, urhs, nps())
                  ps2 = proj(('win', (l, WKR + 1)), 8, urhs, nps())
                  rope_combine(KrT[:, tsl], ps1, ps2)
                  SG = [B16[7 + c] for c in range(4)]
                  for c in range(4):
                      ps = proj(('win', (l, WGM + c)), 8, urhs, nps())
                      sc.act(SG[c].all, ps.all, AF.Silu)
                  qrhs = [qn[c].all for c in range(3)]
                  Qn = [ymix[h] for h in range(4)]
                  for h in range(4):
                      ps = proj(('wqb', (l, h)), 3, qrhs, nps())
                      sc.copy('act', Qn[h].all, ps.all)
                  Qr = [B16[0], B16[1]]
                  for c in range(2):
                      ps1 = proj(('wqb', (l, 4 + c)), 3, qrhs, nps())
                      ps2 = proj(('wqb', (l, 6 + c)), 3, qrhs, nps())
                      rope_combine(Qr[c].all, ps1, ps2)
                  kvrhs = [kvn[c].all for c in range(2)]
                  for h in range(4):
                      ps = proj(('wkvk', (l, h)), 2, kvrhs, nps())
                      sc.copy('act', KnT[h][:, tsl], ps.all)
                  slot = wload(('wkvv', (l,)), 1024)
                  for blk in range(4):
                      ps = nps()
                      for k in range(2):
                          sc.mm(ps.all, kvn[k][:, blk * 128:(blk + 1) * 128], slot[:, k * 512:(k + 1) * 512],
                                start=(k == 0), stop=(k == 1))
                      sc.copy('act' if blk % 2 == 0 else 'dve', Vtok[:, 4 * j + blk, :], ps.all)

                  stage('B')
                  inv_sqrt = 1.0 / math.sqrt(192.0)
                  PT3 = [B16[5], B16[6], B16[11]]
                  psi['n'] = 4
                  for h in range(4):
                      o_ps = PS[4 + 2 * (h % 2)]; den_ps = PS[5 + 2 * (h % 2)]
                      nkb = 4 * j + 4
                      hp = h % 2

                      def qk(kb):
                          i = kb - 4 * j
                          q0 = 128 * i if i > 0 else 0
                          ksl = slice(kb * 128, (kb + 1) * 128)
                          s_ps = nps()
                          sc.mm(s_ps[:, q0:512], KnT[h][:, ksl], Qn[h][:, q0:512], start=True, stop=False)
                          sc.mm(s_ps[:, q0:512], KrT[64 * hp:64 * hp + 64, ksl], Qr[h // 2][64 * hp:64 * hp + 64, q0:512],
                                start=False, stop=True)
                          pt = PT3[kb % 3]
                          sc.act(pt[:, q0:512], s_ps[:, q0:512], AF.Exp, scale=inv_sqrt)
                          if i >= 0:
                              sc.tt('dve', pt[:, q0:q0 + 128], pt[:, q0:q0 + 128], dmask_bf, ALU.mult)

                      def pv(kb):
                          i = kb - 4 * j
                          q0 = 128 * i if i > 0 else 0
                          pt = PT3[kb % 3]
                          sc.mm(o_ps[:, q0:512], Vtok[:, kb, h * 128:(h + 1) * 128], pt[:, q0:512],
                                start=(kb == 0), stop=(kb == nkb - 1))
                          sc.mm(den_ps[:, q0:512], ones_bf, pt[:, q0:512], start=(kb == 0), stop=(kb == nkb - 1))

                      qk(0)
                      for kb in range(nkb):
                          if kb + 1 < nkb:
                              qk(kb + 1)
                          pv(kb)
                      sc.act(G[4][:, 0:512], den_ps.all, AF.Ln)
                      sc.act(G[4][:, 0:512], G[4][:, 0:512], AF.Exp, scale=-1.0)
                      sc.tt('dve', G[5][:, 0:512], o_ps.all, G[4][:, 0:512], ALU.mult)
                      sc.tt('dve', ymix[4 + h].all, G[5][:, 0:512], SG[h].all, ALU.mult)

                  psi['n'] = 6
                  stage('C')
                  for c in range(2):
                      base = 4 * c
                      ps = proj(('win', (l, base + 0)), 8, urhs, nps())
                      sc.copy('act', G[0][:, 0:512], ps.all)
                      ps = proj(('win', (l, base + 1)), 8, urhs, nps())
                      sc.tt('dve', ucv[c][:, 2:514], ps.all, G[0][:, 0:512], ALU.mult)
                      sc.ts('dve', G[1][:, 0:512], ucv[c][:, 2:514], pc(PP_CW + 2 * 2 + c), ALU.mult)
                      sc.stt(G[1][:, 0:512], ucv[c][:, 1:513], pc(PP_CW + 1 * 2 + c), G[1][:, 0:512], ALU.mult, ALU.add)
                      sc.stt(G[1][:, 0:512], ucv[c][:, 0:512], pc(PP_CW + 0 * 2 + c), G[1][:, 0:512], ALU.mult, ALU.add)
                      sc.copy('act', ucv[c][:, 0:2], ucv[c][:, 512:514])
                      ps = proj(('win', (l, base + 2)), 8, urhs, nps())
                      sc.tt('dve', G[2][:, 0:512], ps.all, G[1][:, 0:512], ALU.mult)
                      ps = proj(('win', (l, base + 3)), 8, urhs, nps())
                      sc.act(G[3][:, 0:512], ps.all, AF.Silu)
                      sc.tt('dve', ymix[c].all, G[2][:, 0:512], G[3][:, 0:512], ALU.mult)

                  stage('D')
                  WR = 8
                  Gw = G[7]
                  SGr = [B16[7], B16[8]]

                  def shifted(ps, idx, dst):
                      z = zraw[idx % 2]
                      sc.copy('act', z[:, 1:513], ps.all)
                      sc.copy('act', z[:, 0:1], zhalo[:, idx:idx + 1])
                      sc.copy('act', zhalo[:, idx:idx + 1], z[:, 512:513])
                      sc.tt('dve', G[0][:, 0:512], z[:, 0:512], z[:, 1:513], ALU.subtract)
                      sc.stt(dst, G[0][:, 0:512], pc(PP_MU + idx), z[:, 1:513], ALU.mult, ALU.add)

                  for idx in range(6):
                      ps = proj(('win', (l, WR + idx)), 8, urhs, nps())
                      shifted(ps, idx, X[idx].all)
                  ps = proj(('win', (l, WR + 6)), 8, urhs, nps())
                  shifted(ps, 6, Gw[:, 0:512])
                  for c in range(2):
                      ps = proj(('win', (l, WR + 7 + c)), 8, urhs, nps())
                      sc.act(SGr[c].all, ps.all, AF.Silu)
                  TW = B16[9]
                  sc.act(TW[0:64, :], Gw[0:64, 0:512], AF.Tanh)
                  sc.copy('dve', TW[64:128, :], Gw[64:128, 0:512])

                  stage('E1')
                  def gen_prep(fc):
                      rX, kX, vX = X[fc], X[2 + fc], X[4 + fc]
                      fsl = slice(fc * 128, (fc + 1) * 128)
                      if fc == 0:
                          lw, cl, eig, egx, aT, kx = [G[i][:, 0:512] for i in (0, 1, 3, 4, 5, 6)]
                      else:
                          lw, cl, eig, egx, aT, kx = cosT.all, sinT.all, G[7][:, 0:512], X[6].all, XT[0].all, XT[1].all
                      rs = BV[fc].all
                      sq = B16[fc]
                      prb = B16[5 + fc]
                      eg = EG[fc]
                      AR = ARs[fc]
                      ps_d = nps()
                      sc.mm(ps_d.all, w2a2[0:64, fsl], TW[0:64, :])
                      sc.act(lw, ps_d.all, AF.Sigmoid, bias=pc(PP_W0 + fc))
                      yield
                      ps_a = nps()
                      sc.mm(ps_a.all, w2a2[64:128, fsl], TW[64:128, :])
                      sc.act(aT, ps_a.all, AF.Sigmoid, bias=pc(PP_A0 + fc))
                      yield
                      sc.scan(cl, scanm, lw, 0.0, ALU.mult, ALU.add)
                      yield
                      sc.tt('dve', lw, cl, lw, ALU.subtract)
                      sc.act(eg.all, cl, AF.Exp, scale=-DECAY_SCALE)
                      yield
                      sc.act(eig, cl, AF.Exp, scale=DECAY_SCALE)
                      sc.ts('dve', kx, kX.all, pc(PP_KK + fc), ALU.mult)
                      yield
                      sc.act(egx, lw, AF.Exp, scale=-DECAY_SCALE)
                      yield
                      sc.act(sq.all, kx, AF.Square)
                      ps_s = nps()
                      sc.mm(ps_s.all, bones_bf, sq.all)
                      sc.act(rs, ps_s.all, AF.Ln, bias=eps_ap[1e-12], scale=1.0)
                      yield
                      sc.act(rs, rs, AF.Exp, scale=-0.5)
                      kp = lw
                      sc.ts('dve', kp, aT, -1.0, ALU.add, pc(PP_KA + fc), ALU.mult)
                      yield
                      sc.stt(kp, kp, 1.0, kX.all, ALU.add, ALU.mult)
                      yield
                      sc.tt('dve', kx, kx, rs, ALU.mult)
                      yield
                      BT, KT, BgT, KgT, VbT = RB[fc]
                      sc.stt(AR[:, 0, :], kx, -1.0, egx, ALU.mult, ALU.mult)
                      yield
                      sc.tt('dve', AR[:, 1, :], rX.all, eg.all, ALU.mult)
                      sc.copy('act', VbT.all, vX.all)
                      yield
                      bb = egx
                      sc.tt('dve', bb, kx, aT, ALU.mult)
                      yield
                      sc.tt('dve', BT.all, bb, eig, ALU.mult)
                      yield
                      sc.tt('dve', KT.all, kp, eig, ALU.mult)
                      yield
                      ratio = eig
                      sc.tt('dve', ratio.rr("p (c t) -> p c t", t=64), eig.rr("p (c t) -> p c t", t=64),
                            eg.all.rr("p (c t) -> p c t", t=64)[:, :, 63:64].bc([128, 8, 64]), ALU.mult)
                      yield
                      sc.tt('dve', BgT.all, bb, ratio, ALU.mult)
                      yield
                      sc.tt('dve', KgT.all, kp, ratio, ALU.mult)
                      yield
                      sc.stt(prb.all, rX.all, pc(PP_RK + fc), kp, ALU.mult, ALU.mult)
                      ps_b = nps()
                      sc.mm(ps_b.all, bones_bf, prb.all)
                      sc.tt('dve', BV[fc].all, ps_b.all, vX.all, ALU.mult)
                      yield
                      for blk in range(4):
                          bsl = slice(blk * 128, (blk + 1) * 128)
                          pst = nps()
                          pstb = pst.all.bitcast(BF16)
                          for qi, q in enumerate((AR[:, 0, bsl], BgT[:, bsl], KgT[:, bsl], VbT[:, bsl])):
                              sc.transpose(pstb[:, qi * 128:(qi + 1) * 128], q, ident_bf)
                          sc.copy('act' if blk % 2 else 'dve', tokms[fc][blk].all.rr("p q f -> p (q f)"), pstb[:, 0:512])
                          yield

                  _g = [gen_prep(0), gen_prep(1)]
                  _alive = [True, True]
                  while any(_alive):
                      for _i in range(2):
                          if _alive[_i]:
                              try:
                                  next(_g[_i])
                              except StopIteration:
                                  _alive[_i] = False

                  ypss = [PS[6], PS[7]]
                  psi['n'] = 6
                  for blk in range(4):
                      bsl = slice(blk * 128, (blk + 1) * 128)
                      st = [dict(lc=0, ncur=-1, pcur=0) for _ in range(2)]
                      for fc in range(2):
                          BT, KT, BgT, KgT, VbT = RB[fc]
                          AR = ARs[fc]; C = CM[fc]
                          psNB = nps(); psKB = nps(); psL = nps()
                          for hl in range(2):
                              rows = slice(64 * hl, 64 * hl + 64)
                              sc.mm(psNB[:, hl * 256:(hl + 1) * 256], BT[rows, bsl], AR[rows, :, bsl])
                              sc.mm(psKB[:, hl * 256:(hl + 1) * 256], KT[rows, bsl], AR[rows, :, bsl])
                              sc.mm(psL[:, hl * 128:(hl + 1) * 128], AR[rows, 0, bsl], BT[rows, bsl])
                          sc.tt('dve', C['NBm'].all, psNB.all.rr("p (h c) -> p h c", h=2),
                                mask2_bf.rr("p (o c) -> p o c", o=1).bc([128, 2, 256]), ALU.mult)
                          sc.tt('dve', C['KBm'].all, psKB.all.rr("p (h c) -> p h c", h=2),
                                mask2_bf.rr("p (o c) -> p o c", o=1).bc([128, 2, 256]), ALU.mult)
                          sc.tt('dve', C['Lm'][0].all, psL[:, 0:256].rr("p (h c) -> p h c", h=2),
                                maskTS_bf.rr("p (o c) -> p o c", o=1).bc([128, 2, 128]), ALU.mult)
                          sc.tt('dve', C['Pm'][0].all, C['NBm'][:, :, 0:128],
                                ident_bf.rr("p (o c) -> p o c", o=1).bc([128, 2, 128]), ALU.add)
                      for lev in range(1, 6):
                          for fc in range(2):
                              C = CM[fc]; s_ = st[fc]
                              lc, ncur = s_['lc'], s_['ncur']

                              def Nv(hl, C=C, ncur=ncur):
                                  return C['NBm'][:, hl, 0:128] if ncur < 0 else C['Nm'][ncur][:, hl, :]
                              psL2 = nps()
                              for hl in range(2):
                                  sc.mm(psL2[:, hl * 128:(hl + 1) * 128], Nv(hl), C['Lm'][lc][:, hl, :])
                              sc.copy('act', C['Lm'][1 - lc].all.rr("p h c -> p (h c)"), psL2[:, 0:256])
                              if lev < 5:
                                  psN2 = nps()
                                  for hl in range(2):
                                      sc.mm(psN2[:, hl * 128:(hl + 1) * 128], C['Lm'][lc][:, hl, :], Nv(hl))
                                  nnew = 0 if ncur < 0 else 1 - ncur
                                  sc.copy('dve', C['Nm'][nnew].all.rr("p h c -> p (h c)"), psN2[:, 0:256])
                                  s_['ncur'] = nnew
                              s_['lc'] = 1 - lc
                          for fc in range(2):
                              C = CM[fc]; s_ = st[fc]
                              lc, pcur = s_['lc'], s_['pcur']
                              psP = nps()
                              for hl in range(2):
                                  sc.mm(psP[:, hl * 128:(hl + 1) * 128], C['Lm'][lc][:, hl, :], C['Pm'][pcur][:, hl, :])
                              sc.tt('dve', C['Pm'][1 - pcur].all.rr("p h c -> p (h c)"), psP[:, 0:256],
                                    C['Pm'][pcur].all.rr("p h c -> p (h c)"), ALU.add)
                              s_['pcur'] = 1 - pcur
                      for fc in range(2):
                          C = CM[fc]; tk = tokms[fc][blk]; Tt = C['Pm'][st[fc]['pcur']]
                          psW = nps()
                          for hl in range(2):
                              sc.mm(psW[:, hl * 64:(hl + 1) * 64], C['KBm'][:, hl, 0:128], tk[:, 3, hl * 64:(hl + 1) * 64])
                          sc.copy('act', C['W1b'].all.rr("p h c -> p (h c)"), psW[:, 0:128])
                          psAh = nps()
                          for hl in range(2):
                              rows = slice(64 * hl, 64 * hl + 64)
                              sc.mm(psAh[rows, 0:128], tk[:, 0, hl * 64:(hl + 1) * 64], Tt[:, hl, :])
                          sc.copy('dve', C['AhT'].all, psAh[:, 0:128])
                      for fc in range(2):
                          C = CM[fc]; Tt = C['Pm'][st[fc]['pcur']]
                          psU0 = nps()
                          for hl in range(2):
                              sc.mm(psU0[:, hl * 64:(hl + 1) * 64], Tt[:, hl, :], C['W1b'][:, hl, :])
                          sc.copy('act', C['U0f'].all.rr("p h c -> p (h c)"), psU0[:, 0:128])
                      for ci in range(2):
                          cr = slice(64 * ci, 64 * ci + 64)
                          tcol = slice(blk * 128 + ci * 64, blk * 128 + ci * 64 + 64)
                          gcol = blk * 128 + ci * 64 + 63
                          hbo = [Hb[fc][hcur[fc]] for fc in range(2)]; hfo = [Hf[fc][hcur[fc]] for fc in range(2)]
                          hbn = [Hb[fc][1 - hcur[fc]] for fc in range(2)]; hfn = [Hf[fc][1 - hcur[fc]] for fc in range(2)]
                          for fc in range(2):
                              C = CM[fc]
                              psUc = nps()
                              sc.mm(psUc[cr, 0:128], C['AhT'][:, ci * 64:(ci + 1) * 64], hbo[fc].all)
                              sc.tt('dve', C['Ub'][cr, :, :].rr("p h c -> p (h c)"), psUc[cr, 0:128],
                                    C['U0f'][cr, :, :].rr("p h c -> p (h c)"), ALU.add)
                          psHs = []
                          for fc in range(2):
                              C = CM[fc]; tk = tokms[fc][blk]; AR = ARs[fc]; yps = ypss[fc]
                              sc.mm(yps[:, tcol], hbo[fc].all, AR[:, 1, tcol], start=True, stop=False)
                              for hl in range(2):
                                  rows = slice(64 * hl, 64 * hl + 64)
                                  sc.mm(yps[rows, tcol], C['Ub'][cr, hl, :], C['NBm'][cr, hl, 128 + ci * 64:128 + ci * 64 + 64],
                                        start=False, stop=False)
                                  sc.mm(yps[rows, tcol], tk[cr, 3, hl * 64:(hl + 1) * 64], C['KBm'][cr, hl, 128 + ci * 64:128 + ci * 64 + 64],
                                        start=False, stop=True)
                              psH = nps()
                              psHs.append(psH)
                              for hl in range(2):
                                  rows = slice(64 * hl, 64 * hl + 64)
                                  sc.mm(psH[rows, 0:64], tk[cr, 1, hl * 64:(hl + 1) * 64], C['Ub'][cr, hl, :], start=True, stop=False)
                                  sc.mm(psH[rows, 0:64], tk[cr, 2, hl * 64:(hl + 1) * 64], tk[cr, 3, hl * 64:(hl + 1) * 64],
                                        start=False, stop=True)
                          for fc in range(2):
                              psH = psHs[fc]; eg = EG[fc]
                              for hl in range(2):
                                  rows = slice(64 * hl, 64 * hl + 64)
                                  sc.stt(hbn[fc][rows, hl * 64:(hl + 1) * 64], hfo[fc][rows, :], eg[rows, gcol:gcol + 1], psH[rows, 0:64],
                                         ALU.mult, ALU.add)
                              sc.stt(hfn[fc].all, hfo[fc].all, eg[:, gcol:gcol + 1], psH[:, 0:64], ALU.mult, ALU.add)
                              hcur[fc] = 1 - hcur[fc]

                  yfs = [G[0], G[2]]; dds = [G[1], G[3]]; ybs = [B16[0], B16[1]]; rss = [rstd, G[4]]
                  for fc in range(2):
                      sc.copy('act', yfs[fc][:, 0:512], ypss[fc].all)
                      sc.copy('dve', ybs[fc].all, ypss[fc].all)
                  ps_ms = []
                  for fc in range(2):
                      ps_m = nps(); ps_ms.append(ps_m)
                      sc.mm(ps_m.all, bones_bf, ybs[fc].all)
                  for fc in range(2):
                      sc.stt(dds[fc][:, 0:512], ps_ms[fc].all, -1.0 / 64, yfs[fc][:, 0:512], ALU.mult, ALU.add)
                      sc.act(ybs[fc].all, dds[fc][:, 0:512], AF.Square)
                  ps_vs = []
                  for fc in range(2):
                      ps_v = nps(); ps_vs.append(ps_v)
                      sc.mm(ps_v.all, bones_bf, ybs[fc].all)
                  for fc in range(2):
                      sc.act(rss[fc][:, 0:512], ps_vs[fc].all, AF.Ln, bias=eps_ap[GN_EPS], scale=1.0 / 64)
                  for fc in range(2):
                      sc.act(rss[fc][:, 0:512], rss[fc][:, 0:512], AF.Exp, scale=-0.5)
                  for fc in range(2):
                      dd = dds[fc]
                      sc.tt('dve', dd[:, 0:512], dd[:, 0:512], rss[fc][:, 0:512], ALU.mult)
                      sc.ts('dve', dd[:, 0:512], dd[:, 0:512], pc(PP_GNW + fc), ALU.mult, pc(PP_GNB + fc), ALU.add)
                      sc.tt('dve', dd[:, 0:512], dd[:, 0:512], BV[fc].all, ALU.add)
                      sc.tt('dve', ymix[2 + fc].all, dd[:, 0:512], SGr[fc].all, ALU.mult)

                  stage('E')
                  for i in range(8):
                      sc.dma('sp', G[i][:, 0:512], src[i * 128:(i + 1) * 128, tsl])
                  if j + 1 < NT:
                      nsl = slice(t0 + TT, t0 + 2 * TT)
                      for i in range(8):
                          sc.dma('sp', HN[i].all, src[i * 128:(i + 1) * 128, nsl])
                  yrhs = [ymix[i].all for i in range(8)]
                  for oc in range(8):
                      ps = proj(('wout', (l, oc)), 8, yrhs, nps())
                      sc.tt('dve', G[oc][:, 0:512], ps.all, G[oc][:, 0:512], ALU.add)

                  stage('F')
                  for k in range(2):
                      sc.dma('pool', PB[k].all, pT[l, k * 128:(k + 1) * 128, tsl])
                  psA = nps()
                  for i in range(8):
                      sq = B16[i % 2]
                      sc.act(sq.all, G[i][:, 0:512], AF.Square)
                      sc.mm(psA.all, ones_bf, sq.all, start=(i == 0), stop=(i == 7))
                  rsqrt_bc(rstd.all, psA.all, 1.0 / D_MODEL, NORM_EPS)
                  for i in range(8):
                      sc.stt(uT[i].all, G[i][:, 0:512], pc(PP_G2 + i), rstd.all, ALU.mult, ALU.mult)
                  prhs = [PB[k].all for k in range(2)]
                  for oc in range(8):
                      ps_g = proj(('wgate', (l, oc)), 8, urhs, nps())
                      sg = [cosT, sinT][oc % 2]
                      sc.act(sg.all, ps_g.all, AF.Sigmoid)
                      ps_p = proj(('wple', (l, oc)), 2, prhs, nps())
                      sc.tt('dve', sg.all, ps_p.all, sg.all, ALU.mult)
                      sc.tt('dve', G[oc][:, 0:512], sg.all, G[oc][:, 0:512], ALU.add)

                  if l < L - 1:
                      for i in range(8):
                          sc.dma('sp', hscr[i * 128:(i + 1) * 128, tsl], G[i][:, 0:512])
                  else:
                      psA = nps()
                      for i in range(8):
                          sq = B16[i % 2]
                          sc.act(sq.all, G[i][:, 0:512], AF.Square)
                          sc.mm(psA.all, ones_bf, sq.all, start=(i == 0), stop=(i == 7))
                      rsqrt_bc(rstd.all, psA.all, 1.0 / D_MODEL, NORM_EPS)
                      for i in range(8):
                          sc.stt(G[i][:, 0:512], G[i][:, 0:512], fg[:, i:i + 1], rstd.all, ALU.mult, ALU.mult)
                          sc.dma('sp', outT[i * 128:(i + 1) * 128, tsl], G[i][:, 0:512])

        except _StopBuild:
            pass
        sc.emit(final_wait_bufs=[outT])
        build_program.last_stats = dict(sbuf=sc.sbuf_bytes, counts=dict(sc.count))
    return nc


def _const_matrix():
    cm = np.zeros((128, NCM), np.float32)
    cm[:, CM_ONES:CM_ONES + 128] = 1.0
    idx = np.arange(128)
    same = (idx[:, None] // 64) == (idx[None, :] // 64)
    cm[:, CM_BONES:CM_BONES + 128] = same
    cm[:, CM_IDENT:CM_IDENT + 128] = np.eye(128)
    s = idx[:, None]; t = idx[None, :]
    cm[:, CM_MASK2:CM_MASK2 + 128] = same & (t > s)
    cm[:, CM_MASK2 + 128:CM_MASK2 + 256] = same & (t >= s)
    cm[:, CM_MASKTS:CM_MASKTS + 128] = same & (idx[None, :] < idx[:, None])
    cm[:, CM_DMASK:CM_DMASK + 128] = (idx[:, None] // 64) <= (idx[None, :] // 64)
    sm = np.ones(512, np.float32); sm[::64] = 0.0
    cm[:, CM_SCAN:CM_SCAN + 512] = sm[None, :]
    inv_freq = (1.0 / (np.float32(ROPE_THETA) ** (np.arange(0, 64, 2, dtype=np.float32) / np.float32(64)))).astype(np.float32)
    cm[:, CM_INVF] = inv_freq[idx % 32]
    cm[:, CM_SSIGN] = np.where((idx % 64) < 32, -1.0, 1.0)
    return cm


def _chunk_cols(w, cols, K):
    sub = w[:, cols]
    return np.ascontiguousarray(sub.reshape(K, 128, 128).transpose(1, 0, 2).reshape(128, K * 128))


def _layout_weights(inp, L):
    A = np.arange
    o_cb, o_cc, o_ch, o_cg = 0, 256, 512, 768
    o_r = 1024; o_k = 1280; o_v = 1536; o_wd = 1792; o_ad = 1856; o_rg = 1920
    o_qa = 2176; o_kva = 2560; o_kr = 2816; o_mg = 2880
    chunks = []
    for c in range(2):
        for o in (o_cc, o_ch, o_cb, o_cg):
            chunks.append(o + c * 128 + A(128))
    for o in (o_r, o_r + 128, o_k, o_k + 128, o_v, o_v + 128):
        chunks.append(o + A(128))
    chunks.append(np.concatenate([o_wd + A(64), o_ad + A(64)]))
    chunks.append(o_rg + A(128)); chunks.append(o_rg + 128 + A(128))
    for c in range(3):
        chunks.append(o_qa + c * 128 + A(128))
    for c in range(2):
        chunks.append(o_kva + c * 128 + A(128))
    kr = o_kr + A(64)
    ksw = o_kr + np.concatenate([32 + A(32), A(32)])
    chunks.append(np.concatenate([kr, kr])); chunks.append(np.concatenate([ksw, ksw]))
    for c in range(4):
        chunks.append(o_mg + c * 128 + A(128))
    assert len(chunks) == NWCH
    win = np.stack([np.stack([_chunk_cols(inp["w_in"][l], ch, 8) for ch in chunks]) for l in range(L)])
    w2a2 = np.stack([np.concatenate([inp["rwkv_w2"][l], inp["rwkv_a2"][l]], axis=0) for l in range(L)])
    qch = []
    for h in range(4):
        qch.append(h * 192 + A(128))
    rope = [h * 192 + 128 + A(64) for h in range(4)]
    ropesw = [h * 192 + 128 + np.concatenate([32 + A(32), A(32)]) for h in range(4)]
    qch.append(np.concatenate([rope[0], rope[1]])); qch.append(np.concatenate([rope[2], rope[3]]))
    qch.append(np.concatenate([ropesw[0], ropesw[1]])); qch.append(np.concatenate([ropesw[2], ropesw[3]]))
    wqb = np.stack([np.stack([_chunk_cols(inp["mla_w_qb"][l], ch, 3) for ch in qch]) for l in range(L)])
    wkvk = np.stack([np.stack([_chunk_cols(inp["mla_w_kvb"][l], h * 256 + A(128), 2) for h in range(4)]) for l in range(L)])
    vcols = np.concatenate([h * 256 + 128 + A(128) for h in range(4)])
    wkvv = np.stack([np.ascontiguousarray(
        inp["mla_w_kvb"][l][:, vcols].reshape(2, 128, 512).transpose(1, 0, 2).reshape(128, 1024)) for l in range(L)])
    wout = np.stack([np.stack([_chunk_cols(inp["w_out"][l], oc * 128 + A(128), 8) for oc in range(8)]) for l in range(L)])
    wgate = np.stack([np.stack([_chunk_cols(inp["ple_gate_w"][l], oc * 128 + A(128), 8) for oc in range(8)]) for l in range(L)])
    wple = np.stack([np.stack([_chunk_cols(inp["ple_w"][l], oc * 128 + A(128), 2) for oc in range(8)]) for l in range(L)])
    pp = np.zeros((L, 128, NPP), np.float32)
    for l in range(L):
        cols = []
        cols += [inp["norm_mix_g"][l][i * 128:(i + 1) * 128] for i in range(8)]
        cols += [inp["ple_norm_g"][l][i * 128:(i + 1) * 128] for i in range(8)]
        for tap in range(3):
            for c in range(2):
                cols.append(inp["conv_w"][l][tap, c * 128:(c + 1) * 128])
        mu = inp["rwkv_mu"][l]
        for i in range(6):
            cols.append(mu[i * 128:(i + 1) * 128])
        cols.append(mu[768:896])
        for nm in ("rwkv_w0", "rwkv_a0", "rwkv_kk", "rwkv_ka"):
            for c in range(2):
                cols.append(inp[nm][l][c * 128:(c + 1) * 128])
        rk = inp["rwkv_rk"][l].reshape(256)
        for c in range(2):
            cols.append(rk[c * 128:(c + 1) * 128])
        for nm in ("rwkv_gn_w", "rwkv_gn_b"):
            for c in range(2):
                cols.append(inp[nm][l][c * 128:(c + 1) * 128])
        for c in range(3):
            cols.append(inp["mla_q_norm_g"][l][c * 128:(c + 1) * 128])
        for c in range(2):
            cols.append(inp["mla_kv_norm_g"][l][c * 128:(c + 1) * 128])
        assert len(cols) == NPP
        pp[l] = np.stack(cols, axis=1)
    fg = np.ascontiguousarray(inp["final_norm_g"].reshape(8, 128).T)
    return dict(win=win, w2a2=w2a2, wqb=wqb, wkvk=wkvk, wkvv=wkvv, wout=wout, wgate=wgate, wple=wple, pp=pp, fg=fg)


_PROG_CACHE = {}


def run_cores(inp, S, L, nb, n_cores):
    inp = {k: np.asarray(v) for k, v in inp.items()}
    wl = _layout_weights(inp, L)
    cmat = _const_matrix()
    key = (S, L)
    if key not in _PROG_CACHE:
        build_program(S, L)
        _PROG_CACHE[key] = build_program(S, L, wseq=list(build_program.wrec))
    nc = _PROG_CACHE[key]
    in_maps = []
    for c in range(n_cores):
        b = c % nb
        m = dict(wl)
        m["cmat"] = cmat
        m["xT"] = np.ascontiguousarray(inp["x"][b].T)
        m["pT"] = np.ascontiguousarray(np.transpose(inp["p"][:, b], (0, 2, 1)))
        m["pos"] = np.ascontiguousarray(np.broadcast_to(inp["positions"][b][None, :].astype(np.int32), (128, S)))
        in_maps.append(m)
    res = run_bass_kernel_spmd(nc, in_maps, core_ids=list(range(n_cores)))
    out = np.stack([np.ascontiguousarray(res.results[b]["outT"].T) for b in range(nb)])
    return out.astype(np.float32)


def kernel(**inputs):
    return run_cores(inputs, SEQ, DEPTH, BATCH, 8)
```
